# Optimizing a Trainium2 kernel written in Bass

```python
import math
import jax, jax.numpy as jnp
from jax import lax
import numpy as np

D_MODEL = 1024
BATCH = 8
SEQ = 2048
DEPTH = 2

MEM_LEN = 256
N_HEADS_GROUP = 4
HEAD_DIM = 64
GROUP_WIDTH = N_HEADS_GROUP * HEAD_DIM
N_GROUPS = 4
D_MIX = N_GROUPS * GROUP_WIDTH
ROPE_THETA = 10000.0
EPS = 1e-6
Q_BLOCK = 128
NEG_INF = -1e30
BIG = 1e30

NSA_CMP_BLOCK = 32
NSA_CMP_STRIDE = 16
NSA_SLC_BLOCK = 64
NSA_N_SELECT = 16
NSA_N_LOCAL = 2
NSA_WINDOW = 512
NSA_N_BRANCH = 3

DIFF_D = HEAD_DIM // 2

MLA_Q_RANK = 256
MLA_KV_RANK = 128
MLA_NOPE = 64
MLA_ROPE = 32
MLA_V = HEAD_DIM
MLA_QK = MLA_NOPE + MLA_ROPE

IN_SIZES = (
    GROUP_WIDTH, 6 * HEAD_DIM, NSA_N_BRANCH * N_HEADS_GROUP, GROUP_WIDTH,
    3 * GROUP_WIDTH, GROUP_WIDTH,
    MLA_Q_RANK, MLA_KV_RANK, MLA_ROPE, GROUP_WIDTH,
    GROUP_WIDTH, GROUP_WIDTH,
)
D_IN = sum(IN_SIZES)

kernel_name = 'hybrid_nsa_diff_mla_memory_block'


def rms_norm(x, g):
    xf = x.astype(jnp.float32)
    y = xf * lax.rsqrt(jnp.mean(xf * xf, axis=-1, keepdims=True) + EPS)
    return (y * g.astype(jnp.float32)).astype(x.dtype)


def rope(x, pos):
    half = x.shape[-1] // 2
    inv_freq = ROPE_THETA ** (-jnp.arange(half, dtype=jnp.float32) / half)
    ang = pos.astype(jnp.float32)[:, None] * inv_freq[None, :]
    cos = jnp.cos(ang)[:, None, :]
    sin = jnp.sin(ang)[:, None, :]
    xf = x.astype(jnp.float32)
    x1, x2 = xf[..., :half], xf[..., half:]
    return jnp.concatenate([x1 * cos - x2 * sin, x2 * cos + x1 * sin], axis=-1).astype(x.dtype)


def masked_softmax(s, mask):
    s = jnp.where(mask, s.astype(jnp.float32), NEG_INF)
    p = jax.nn.softmax(s, axis=-1)
    return jnp.where(mask, p, 0.0)


def causal_attention(q, k, v):
    B, S, H, Dk = q.shape
    nb = S // Q_BLOCK
    scale = Dk ** -0.5
    q_blocks = q.reshape(B, nb, Q_BLOCK, H, Dk).transpose(1, 0, 2, 3, 4)
    pos_blocks = jnp.arange(S).reshape(nb, Q_BLOCK)
    kpos = jnp.arange(S)

    def one_block(args):
        qi, qpos = args
        s = jnp.einsum('bqhd,bkhd->bhqk', qi, k).astype(jnp.float32) * scale
        p = masked_softmax(s, (kpos[None, :] <= qpos[:, None])[None, None])
        return jnp.einsum('bhqk,bkhd->bqhd', p.astype(v.dtype), v)

    o = lax.map(one_block, (q_blocks, pos_blocks))
    return o.transpose(1, 0, 2, 3, 4).reshape(B, S, H, v.shape[-1])


def nsa_mixer(q, kv, gate_logits, qk_gain, cmp_pe, w_cmp, pos):
    B, S, H, D = q.shape
    scale = D ** -0.5
    kc, vc, ks, vs, kw, vw = jnp.split(kv, 6, axis=-1)
    q = rope(rms_norm(q, qk_gain[0]), pos)

    nc = (S - NSA_CMP_BLOCK) // NSA_CMP_STRIDE + 1
    starts = jnp.arange(nc) * NSA_CMP_STRIDE
    idx = starts[:, None] + jnp.arange(NSA_CMP_BLOCK)[None, :]
    k_cmp = (kc[:, idx] + cmp_pe[0]).reshape(B, nc, NSA_CMP_BLOCK * D) @ w_cmp[0]
    v_cmp = (vc[:, idx] + cmp_pe[1]).reshape(B, nc, NSA_CMP_BLOCK * D) @ w_cmp[1]
    cmp_end = starts + NSA_CMP_BLOCK - 1
    k_cmp = rope(rms_norm(k_cmp, qk_gain[1])[:, :, None, :], cmp_end)[:, :, 0]
    s_cmp = jnp.einsum('bshd,bnd->bhsn', q, k_cmp).astype(jnp.float32) * scale
    p_cmp = masked_softmax(s_cmp, (cmp_end[None, :] <= pos[:, None])[None, None])
    o_cmp = jnp.einsum('bhsn,bnd->bshd', p_cmp.astype(v_cmp.dtype), v_cmp)

    ns = S // NSA_SLC_BLOCK
    ratio = NSA_SLC_BLOCK // NSA_CMP_STRIDE
    coef = np.convolve(np.ones(ratio), np.ones(NSA_CMP_BLOCK // NSA_CMP_STRIDE))
    need = ratio * (ns - 1) + len(coef)
    p_g = jnp.pad(p_cmp.sum(axis=1), ((0, 0), (0, 0), (0, need - nc)))
    p_slc = sum(float(c) * p_g[..., i: i + ratio * (ns - 1) + 1: ratio] for i, c in enumerate(coef))
    blk = jnp.arange(ns)[None, :]
    cur = (pos // NSA_SLC_BLOCK)[:, None]
    forced = (blk == 0) | ((blk <= cur) & (blk > cur - NSA_N_LOCAL))
    score = jnp.where(blk > cur, NEG_INF, jnp.where(forced, BIG, p_slc))
    n_sel = min(NSA_N_SELECT, ns)
    _, sel = lax.top_k(score, n_sel)

    ks = rope(rms_norm(ks, qk_gain[2])[:, :, None, :], pos)[:, :, 0]
    nb = S // Q_BLOCK
    q_blocks = q.reshape(B, nb, Q_BLOCK, H, D).transpose(1, 0, 2, 3, 4)
    sel_blocks = sel.reshape(B, nb, Q_BLOCK, n_sel).transpose(1, 0, 2, 3)
    pos_blocks = pos.reshape(nb, Q_BLOCK)
    in_block = jnp.arange(NSA_SLC_BLOCK)
    gather = jax.vmap(lambda t, i: t[i])

    def select_block(args):
        qi, si, qpos = args
        tok = (si[..., None] * NSA_SLC_BLOCK + in_block).reshape(B, Q_BLOCK, n_sel * NSA_SLC_BLOCK)
        k_sel = gather(ks, tok)
        v_sel = gather(vs, tok)
        s = jnp.einsum('bqhd,bqkd->bhqk', qi, k_sel).astype(jnp.float32) * scale
        p = masked_softmax(s, (tok <= qpos[None, :, None])[:, None])
        return jnp.einsum('bhqk,bqkd->bqhd', p.astype(v_sel.dtype), v_sel)

    o_slc = lax.map(select_block, (q_blocks, sel_blocks, pos_blocks))
    o_slc = o_slc.transpose(1, 0, 2, 3, 4).reshape(B, S, H, D)

    kw = rope(rms_norm(kw, qk_gain[3])[:, :, None, :], pos)[:, :, 0]
    nw = NSA_WINDOW // Q_BLOCK

    def band(t):
        tp = jnp.pad(t, ((0, 0), (NSA_WINDOW, 0), (0, 0))).reshape(B, nb + nw, Q_BLOCK, D)
        return jnp.concatenate([tp[:, i: i + nb] for i in range(nw + 1)], axis=2)

    kb, vb = band(kw), band(vw)
    qb = q.reshape(B, nb, Q_BLOCK, H, D)
    s_win = jnp.einsum('bnqhd,bnkd->bnhqk', qb, kb).astype(jnp.float32) * scale
    kpos = (jnp.arange(nb)[:, None] - nw) * Q_BLOCK + jnp.arange((nw + 1) * Q_BLOCK)[None, :]
    dist = pos_blocks[:, :, None] - kpos[:, None, :]
    win_mask = (kpos[:, None, :] >= 0) & (dist >= 0) & (dist < NSA_WINDOW)
    p_win = masked_softmax(s_win, win_mask[None, :, None])
    o_win = jnp.einsum('bnhqk,bnkd->bnqhd', p_win.astype(vb.dtype), vb).reshape(B, S, H, D)

    g = jax.nn.sigmoid(gate_logits.astype(jnp.float32)).reshape(B, S, NSA_N_BRANCH, H)[..., None]
    o = g[:, :, 0] * o_cmp + g[:, :, 1] * o_slc + g[:, :, 2] * o_win
    return o.astype(q.dtype).reshape(B, S, H * D)


def diff_mixer(q, k, v, qk_gain, lam, subln_gain, lambda_init, pos):
    B, S, _ = q.shape
    q = rope(rms_norm(q.reshape(B, S, N_HEADS_GROUP * 2, DIFF_D), qk_gain[0]), pos)
    k = rope(rms_norm(k.reshape(B, S, N_HEADS_GROUP * 2, DIFF_D), qk_gain[1]), pos)
    q = q.reshape(B, S, N_HEADS_GROUP, 2, DIFF_D)
    k = k.reshape(B, S, N_HEADS_GROUP, 2, DIFF_D)
    v = v.reshape(B, S, N_HEADS_GROUP, HEAD_DIM)
    o1 = causal_attention(q[:, :, :, 0], k[:, :, :, 0], v)
    o2 = causal_attention(q[:, :, :, 1], k[:, :, :, 1], v)
    lf = lam.astype(jnp.float32)
    lmbda = jnp.exp(jnp.sum(lf[0] * lf[1])) - jnp.exp(jnp.sum(lf[2] * lf[3])) + lambda_init
    o = o1.astype(jnp.float32) - lmbda * o2.astype(jnp.float32)
    o = rms_norm(o, subln_gain) * (1.0 - lambda_init)
    return o.astype(v.dtype).reshape(B, S, GROUP_WIDTH)


def mla_mixer(c_q, c_kv, k_rope, cq_gain, ckv_gain, w_uq, w_ukv, qk_gain, pos):
    B, S, _ = c_q.shape
    q = (rms_norm(c_q, cq_gain) @ w_uq).reshape(B, S, N_HEADS_GROUP, MLA_QK)
    kv = (rms_norm(c_kv, ckv_gain) @ w_ukv).reshape(B, S, N_HEADS_GROUP, MLA_NOPE + MLA_V)
    q = jnp.concatenate([q[..., :MLA_NOPE], rope(q[..., MLA_NOPE:], pos)], axis=-1)
    kr = rope(k_rope[:, :, None, :], pos)
    k = jnp.concatenate([kv[..., :MLA_NOPE], jnp.broadcast_to(kr, (B, S, N_HEADS_GROUP, MLA_ROPE))], axis=-1)
    v = kv[..., MLA_NOPE:]
    q = rms_norm(q, qk_gain[0])
    k = rms_norm(k, qk_gain[1])
    return causal_attention(q, k, v).reshape(B, S, GROUP_WIDTH)


def memory_mixer(q, mem, mem_gain, w_kv, qk_gain):
    B, S, _ = q.shape
    M = mem.shape[1]
    q = rms_norm(q.reshape(B, S, N_HEADS_GROUP, HEAD_DIM), qk_gain[0])
    k, v = jnp.split(rms_norm(mem, mem_gain) @ w_kv, 2, axis=-1)
    k = rms_norm(k.reshape(B, M, N_HEADS_GROUP, HEAD_DIM), qk_gain[1])
    v = v.reshape(B, M, N_HEADS_GROUP, HEAD_DIM)
    s = jnp.einsum('bshd,bmhd->bhsm', q, k).astype(jnp.float32) * HEAD_DIM ** -0.5
    p = jax.nn.softmax(s, axis=-1)
    return jnp.einsum('bhsm,bmhd->bshd', p.astype(v.dtype), v).reshape(B, S, GROUP_WIDTH)


def hybrid_layer(x, mem, layer_idx, norm_gain, w_in, w_out, nsa_qk_gain, nsa_cmp_pe, nsa_w_cmp,
                 diff_qk_gain, diff_lambda, diff_subln_gain, mla_cq_gain, mla_ckv_gain, mla_w_uq,
                 mla_w_ukv, mla_qk_gain, mem_norm_gain, mem_w_kv, mem_qk_gain):
    B, S, _ = x.shape
    pos = jnp.arange(S, dtype=jnp.int32)
    h = rms_norm(x, norm_gain)
    u = h @ w_in
    offsets = [int(o) for o in np.cumsum(IN_SIZES)[:-1]]
    (nsa_q, nsa_kv, nsa_gl, nsa_z, diff_qkv, diff_z,
     mla_cq, mla_ckv, mla_kr, mla_z, mem_q, mem_z) = jnp.split(u, offsets, axis=-1)

    y_nsa = nsa_mixer(nsa_q.reshape(B, S, N_HEADS_GROUP, HEAD_DIM), nsa_kv, nsa_gl,
                      nsa_qk_gain, nsa_cmp_pe, nsa_w_cmp, pos)
    dq, dk, dv = jnp.split(diff_qkv, 3, axis=-1)
    lambda_init = 0.8 - 0.6 * math.exp(-0.3 * layer_idx)
    y_diff = diff_mixer(dq, dk, dv, diff_qk_gain, diff_lambda, diff_subln_gain, lambda_init, pos)
    y_mla = mla_mixer(mla_cq, mla_ckv, mla_kr, mla_cq_gain, mla_ckv_gain, mla_w_uq, mla_w_ukv,
                      mla_qk_gain, pos)
    y_mem = memory_mixer(mem_q, mem, mem_norm_gain, mem_w_kv, mem_qk_gain)

    y = jnp.concatenate([y_nsa * jax.nn.silu(nsa_z), y_diff * jax.nn.silu(diff_z),
                         y_mla * jax.nn.silu(mla_z), y_mem * jax.nn.silu(mem_z)], axis=-1)
    return x + y @ w_out


def setup_inputs(seed: int = 0) -> dict:
    key = jax.random.key(seed)
    k = jax.random.split(key, 19)
    f32 = jnp.float32

    def dense(kk, shape, fan_in):
        return jax.random.normal(kk, shape, f32) * fan_in ** -0.5

    def gain(kk, shape):
        return 1.0 + 0.02 * jax.random.normal(kk, shape, f32)

    L = DEPTH
    return {
        'x': jax.random.normal(k[0], (BATCH, SEQ, D_MODEL), f32),
        'mem': jax.random.normal(k[1], (BATCH, MEM_LEN, D_MODEL), f32),
        'norm_gain': gain(k[2], (L, D_MODEL)),
        'w_in': dense(k[3], (L, D_MODEL, D_IN), D_MODEL),
        'w_out': dense(k[4], (L, D_MIX, D_MODEL), D_MIX),
        'nsa_qk_gain': gain(k[5], (L, 4, HEAD_DIM)),
        'nsa_cmp_pe': 0.1 * jax.random.normal(k[6], (L, 2, NSA_CMP_BLOCK, HEAD_DIM), f32),
        'nsa_w_cmp': dense(k[7], (L, 2, NSA_CMP_BLOCK * HEAD_DIM, HEAD_DIM), NSA_CMP_BLOCK * HEAD_DIM),
        'diff_qk_gain': gain(k[8], (L, 2, DIFF_D)),
        'diff_lambda': 0.1 * jax.random.normal(k[9], (L, 4, DIFF_D), f32),
        'diff_subln_gain': gain(k[10], (L, HEAD_DIM)),
        'mla_cq_gain': gain(k[11], (L, MLA_Q_RANK)),
        'mla_ckv_gain': gain(k[12], (L, MLA_KV_RANK)),
        'mla_w_uq': dense(k[13], (L, MLA_Q_RANK, N_HEADS_GROUP * MLA_QK), MLA_Q_RANK),
        'mla_w_ukv': dense(k[14], (L, MLA_KV_RANK, N_HEADS_GROUP * (MLA_NOPE + MLA_V)), MLA_KV_RANK),
        'mla_qk_gain': gain(k[15], (L, 2, MLA_QK)),
        'mem_norm_gain': gain(k[16], (L, D_MODEL)),
        'mem_w_kv': dense(k[17], (L, D_MODEL, 2 * GROUP_WIDTH), D_MODEL),
        'mem_qk_gain': gain(k[18], (L, 2, HEAD_DIM)),
    }


def reference(x, mem, norm_gain, w_in, w_out, nsa_qk_gain, nsa_cmp_pe, nsa_w_cmp, diff_qk_gain,
              diff_lambda, diff_subln_gain, mla_cq_gain, mla_ckv_gain, mla_w_uq, mla_w_ukv,
              mla_qk_gain, mem_norm_gain, mem_w_kv, mem_qk_gain):
    for l in range(DEPTH):
        x = hybrid_layer(x, mem, l, norm_gain[l], w_in[l], w_out[l], nsa_qk_gain[l], nsa_cmp_pe[l],
                         nsa_w_cmp[l], diff_qk_gain[l], diff_lambda[l], diff_subln_gain[l],
                         mla_cq_gain[l], mla_ckv_gain[l], mla_w_uq[l], mla_w_ukv[l], mla_qk_gain[l],
                         mem_norm_gain[l], mem_w_kv[l], mem_qk_gain[l])
    return x
```

```python
import numpy as np
import concourse.bass as bass
import concourse.mybir as mybir
from concourse.bass_utils import run_bass_kernel_spmd

F32 = mybir.dt.float32
BF = mybir.dt.bfloat16
I32 = mybir.dt.int32
ALU = mybir.AluOpType
AF = mybir.ActivationFunctionType
AX = mybir.AxisListType

ENGS = ('pe', 'act', 'dve', 'pool', 'sp')
EPOCH = 3000


class Op:
    __slots__ = ('eng', 'fn', 'deps', 'is_dma', 'idx', 'gidx', 'need_inc', 'sem', 'val',
                 'waits', 'vc', 'dma_prev')

    def __init__(self, eng, fn, is_dma):
        self.eng = eng
        self.fn = fn
        self.is_dma = is_dma
        self.deps = {}
        self.need_inc = False
        self.sem = None
        self.val = 0
        self.waits = []
        self.vc = None
        self.dma_prev = None


def _bbox(ap):
    sp = str(ap.space)
    if 'SB' not in sp and 'PSUM' not in sp:
        return None
    pat = ap.ap
    pstride, npart = pat[0]
    off = ap.offset
    if pstride == 0:
        ts = ap.tensor.shape
        row = 1
        for s in ts[1:]:
            row *= s
        pstride_eff = row
    else:
        pstride_eff = pstride
    p0 = off // pstride_eff
    f0 = off % pstride_eff
    ext = 0
    for st, cnt in pat[1:]:
        ext += abs(st) * (cnt - 1)
    return (ap.tensor.name, 'PSUM' in sp, p0, p0 + npart, f0, f0 + ext + 1)


class Prog:
    def __init__(self, nc):
        self.nc = nc
        self.ops = {e: [] for e in ENGS}
        self.all_ops = []
        self.track = {}
        self.tmp_pool = {}
        self.tmp_ctr = {}
        self._stack = []

    def sb(self, name, shape, dtype):
        cm = self.nc.sbuf_tensor(name, list(shape), dtype)
        t = cm.__enter__()
        self._stack.append(cm)
        return t[:]

    def ps(self, name, shape, dtype):
        cm = self.nc.psum_tensor(name, list(shape), dtype)
        t = cm.__enter__()
        self._stack.append(cm)
        return t[:]

    def tmp(self, name, shape, dtype, nbuf=2):
        key = name
        if key not in self.tmp_pool:
            self.tmp_pool[key] = [self.sb(f"{name}_{i}", shape, dtype) for i in range(nbuf)]
            self.tmp_ctr[key] = 0
        i = self.tmp_ctr[key]
        self.tmp_ctr[key] = (i + 1) % len(self.tmp_pool[key])
        return self.tmp_pool[key][i]

    def op(self, eng, fn, reads=(), writes=(), dma=False):
        o = Op(eng, fn, dma)
        o.gidx = len(self.all_ops)
        for ap in reads:
            self._access(o, ap, 'r')
        for ap in writes:
            self._access(o, ap, 'w')
        o.idx = len(self.ops[eng])
        self.ops[eng].append(o)
        self.all_ops.append(o)
        return o

    def _access(self, o, ap, kind):
        bb = _bbox(ap)
        if bb is None:
            return
        name, is_psum, p0, p1, f0, f1 = bb
        lst = self.track.setdefault(name, [])
        if is_psum:
            for e in lst:
                d = e[4]
                if d is o:
                    continue
                if d.eng != o.eng or d.is_dma or o.is_dma:
                    o.deps[d] = o.deps.get(d, False) or (kind == 'r' and e[5] == 'w')
                elif kind == 'r' and e[5] == 'w' and not (p1 <= e[0] or e[1] <= p0 or f1 <= e[2] or e[3] <= f0):
                    o.deps[d] = True
            lst[:] = [e for e in lst if not (e[4].eng == o.eng and e[5] == kind and not e[4].is_dma)]
            lst.append([p0, p1, f0, f1, o, kind])
            return
        keep = []
        for e in lst:
            d = e[4]
            ov = not (p1 <= e[0] or e[1] <= p0 or f1 <= e[2] or e[3] <= f0)
            if ov and d is not o:
                if kind == 'r':
                    if e[5] == 'w':
                        o.deps[d] = True
                else:
                    o.deps[d] = o.deps.get(d, False)
            if kind == 'w' and e[0] >= p0 and e[1] <= p1 and e[2] >= f0 and e[3] <= f1 and d is not o:
                continue
            if kind == 'r' and e[5] == 'r' and d.eng == o.eng and not d.is_dma and not o.is_dma \
                    and e[0] == p0 and e[1] == p1 and e[2] == f0 and e[3] == f1:
                continue
            keep.append(e)
        keep.append([p0, p1, f0, f1, o, kind])
        lst[:] = keep

    def mm(self, out, lhsT, rhs, start=True, stop=True):
        return self.op('pe', lambda e: e.matmul(out, lhsT, rhs, start=start, stop=stop),
                       reads=[lhsT, rhs], writes=[out])

    def tr(self, out, in_, ident):
        return self.op('pe', lambda e: e.transpose(out, in_, ident), reads=[in_, ident], writes=[out])

    def act(self, out, in_, func, scale=1.0, bias=0.0, accum=None):
        rd = [in_]
        if not isinstance(scale, (int, float)):
            rd.append(scale)
        if not isinstance(bias, (int, float)):
            rd.append(bias)
        wr = [out] + ([accum] if accum is not None else [])
        kw = {}
        if accum is not None:
            kw['accum_out'] = accum
        return self.op('act', lambda e: e.activation(out, in_, func, bias=bias, scale=scale, **kw),
                       reads=rd, writes=wr)

    def _e(self, eng):
        return eng

    def tt(self, eng, out, in0, in1, op):
        return self.op(eng, lambda e: e.tensor_tensor(out, in0, in1, op), reads=[in0, in1], writes=[out])

    def ts(self, eng, out, in0, s1, s2, op0, op1=None):
        rd = [in0]
        if not isinstance(s1, (int, float)):
            rd.append(s1)
        if s2 is not None and not isinstance(s2, (int, float)):
            rd.append(s2)
        if op1 is None:
            return self.op(eng, lambda e: e.tensor_scalar(out, in0, s1, None, op0), reads=rd, writes=[out])
        return self.op(eng, lambda e: e.tensor_scalar(out, in0, s1, s2, op0, op1), reads=rd, writes=[out])

    def stt(self, out, in0, scalar, in1, op0, op1):
        rd = [in0, in1]
        if not isinstance(scalar, (int, float)):
            rd.append(scalar)
        return self.op('dve', lambda e: e.scalar_tensor_tensor(out, in0, scalar, in1, op0, op1),
                       reads=rd, writes=[out])

    def copy(self, eng, out, in_):
        if eng == 'act':
            return self.op('act', lambda e: e.copy(out, in_), reads=[in_], writes=[out])
        return self.op(eng, lambda e: e.tensor_copy(out, in_), reads=[in_], writes=[out])

    def reduce(self, out, in_, op=None, axis=None):
        op = op or ALU.add
        axis = axis or AX.X
        return self.op('dve', lambda e: e.tensor_reduce(out, in_, axis, op), reads=[in_], writes=[out])

    def recip(self, out, in_):
        return self.op('dve', lambda e: e.reciprocal(out, in_), reads=[in_], writes=[out])

    def memset(self, eng, ap, val):
        return self.op(eng, lambda e: e.memset(ap, val), writes=[ap])

    def iota(self, out, pattern, base, cm):
        return self.op('pool', lambda e: e.iota(out, pattern, base=base, channel_multiplier=cm,
                                                 allow_small_or_imprecise_dtypes=True), writes=[out])

    def asel(self, out, in_, pattern, cmp, fill, base, cm):
        return self.op('pool', lambda e: e.affine_select(out, in_, pattern, cmp, fill, base=base,
                                                          channel_multiplier=cm), reads=[in_], writes=[out])

    def max8(self, out, in_):
        return self.op('dve', lambda e: e.max(out, in_), reads=[in_], writes=[out])

    def match_replace(self, out, to_rep, vals, imm):
        return self.op('dve', lambda e: e.match_replace(out, to_rep, vals, imm), reads=[to_rep, vals], writes=[out])

    def dma(self, q, out, in_, **kw):
        return self.op(q, lambda e: e.dma_start(out=out, in_=in_, **kw), reads=[in_], writes=[out], dma=True)

    def finalize(self):
        nc = self.nc
        eidx = {e: i for i, e in enumerate(ENGS)}
        NDMA = 24
        known = {e: [-1] * len(ENGS) for e in ENGS}
        dma_known = {e: set() for e in ENGS}
        dma_rr = {e: 0 for e in ENGS}
        dma_last = {}
        for o in self.all_ops:
            E = o.eng
            kn = known[E]
            dk = dma_known[E]
            waits = []
            for d, raw in sorted(o.deps.items(), key=lambda kv: kv[0].gidx):
                if d.is_dma:
                    if d in dk:
                        continue
                    waits.append(d)
                    dk.add(d)
                    if d.vc is not None:
                        for i in range(len(ENGS)):
                            if d.vc[i] > kn[i]:
                                kn[i] = d.vc[i]
                    continue
                if d.eng == E and not o.is_dma:
                    if E == 'pe':
                        continue
                    if kn[eidx[E]] >= d.idx:
                        continue
                    waits.append(d)
                    kn[eidx[E]] = d.idx
                    continue
                if kn[eidx[d.eng]] >= d.idx:
                    continue
                waits.append(d)
                for i in range(len(ENGS)):
                    if d.vc[i] > kn[i]:
                        kn[i] = d.vc[i]
                if d.idx > kn[eidx[d.eng]]:
                    kn[eidx[d.eng]] = d.idx
            for d in waits:
                d.need_inc = True
            o.waits = waits
            if o.is_dma:
                slot = dma_rr[E]
                dma_rr[E] = (slot + 1) % NDMA
                o.sem = (E, slot)
                prev = dma_last.get((E, slot))
                o.dma_prev = prev
                dma_last[(E, slot)] = o
                o.need_inc = True
                o.vc = list(kn)
            else:
                vc = list(kn)
                vc[eidx[E]] = o.idx
                o.vc = vc
        sems = {}

        def get_sem(key):
            if key not in sems:
                cm = nc.semaphore(f"s_{key[0]}_{key[1]}")
                sems[key] = cm.__enter__()
                self._stack.append(cm)
            return sems[key]

        cnt = {e: 0 for e in ENGS}
        dma_tot = {}
        for o in self.all_ops:
            if o.is_dma:
                key = ('d' + o.sem[0], o.sem[1])
                dma_tot[key] = dma_tot.get(key, 0) + 16
                o.sem = key
                o.val = dma_tot[key]
            elif o.need_inc:
                c = cnt[o.eng]
                o.sem = (o.eng, c // EPOCH)
                o.val = c % EPOCH + 1
                cnt[o.eng] = c + 1
        self.nsems = None
        engmap = {'pe': 'tensor', 'act': 'scalar', 'dve': 'vector', 'pool': 'gpsimd', 'sp': 'sync'}
        for o in self.all_ops:
            if o.sem is not None:
                get_sem(o.sem)
        prog = self

        def emit(engname):
            def f(eng):
                for o in prog.ops[engname]:
                    for d in o.waits:
                        eng.wait_ge(sems[d.sem], d.val)
                    if o.is_dma and o.dma_prev is not None:
                        eng.wait_ge(sems[o.sem], o.dma_prev.val)
                    ins = o.fn(eng)
                    if o.is_dma:
                        ins.then_inc(sems[o.sem], 16)
                    elif o.need_inc:
                        ins.then_inc(sems[o.sem], 1)
                last = {}
                for o in prog.ops[engname]:
                    if o.is_dma:
                        last[o.sem] = o.val
                for k, v in last.items():
                    eng.wait_ge(sems[k], v)
            return f

        with nc.Block() as block:
            block.tensor(emit('pe'))
            block.scalar(emit('act'))
            block.vector(emit('dve'))
            block.gpsimd(emit('pool'))
            block.sync(emit('sp'))
        n = {e: len(self.ops[e]) for e in ENGS}
        w = {e: sum(len(o.waits) for o in self.ops[e]) for e in ENGS}
        print("ops per engine", n, "waits", w, "incs", cnt, "nsems", len(sems))

    def close(self):
        while self._stack:
            cm = self._stack.pop()
            cm.__exit__(None, None, None)

import math
import os
NOSEL = os.environ.get('NSA_NOSEL') == '1'
NOCMPKV = os.environ.get('NSA_NOCMPKV') == '1'

S = 2048
D = 1024
NT = 16
DEPTH = 2
EPS = 1e-6
NEGB = -30000.0
THETA = 10000.0
TWO_PI = 2.0 * math.pi


def bc_last(ap2, d):
    s = list(ap2.shape)
    return ap2.unsqueeze(len(s)).to_broadcast(s + [d])


def bc_mid(ap2, h):
    s = list(ap2.shape)
    return ap2.unsqueeze(1).to_broadcast([s[0], h] + s[1:])


def v3(ap, d):
    return ap.rearrange("p (h d) -> p h d", d=d)


class Model:
    def __init__(self, nc, debug=None, nlayers=DEPTH, groups=(0, 1, 2, 3)):
        self.nc = nc
        self.debug = debug or {}
        self.nlayers = nlayers
        self.groups = groups
        self.P = Prog(nc)
        self.dbg_out = {}
        self.psum_pools = {}
        self.psum_ctr = {}
        self.win_loaded = None

    def psum_next(self, pool):
        if pool in ('proj', 'tr'):
            b = self.psum_u[self.psum_uctr]
            self.psum_uctr = (self.psum_uctr + 1) % len(self.psum_u)
            return b.bitcast(BF) if pool == 'tr' else b
        i = self.psum_ctr[pool]
        self.psum_ctr[pool] = (i + 1) % len(self.psum_pools[pool])
        return self.psum_pools[pool][i]

    def dump(self, name, ap, dtype=F32):
        if name not in self.debug:
            return
        shp = list(ap.shape)
        d = self.nc.dram_tensor("dbg_" + name, shp, dtype, kind="ExternalOutput").ap()
        self.P.dma('sp', d, ap)
        self.dbg_out[name] = "dbg_" + name

    def rms(self, src, H, Dh, gain, out, npart=128, out3=None):
        P = self.P
        n = H * Dh
        sq = P.tmp('sq', [128, 384], F32, 1)
        P.act(sq[:npart, :n], src, AF.Square)
        ss = P.tmp('ss', [128, 8], F32, 3)
        P.reduce(ss[:npart, :H], v3(sq[:npart, :n], Dh))
        ms = P.tmp('ms', [128, 8], F32, 3)
        P.ts('dve', ms[:npart, :H], ss[:npart, :H], 1.0 / Dh, EPS, ALU.mult, ALU.add)
        rs = P.tmp('rs', [128, 8], F32, 3)
        P.tt('pool', rs[:npart, :H], ms[:npart, :H], self.neghalf[:npart, :H], ALU.pow)
        if out3 is None:
            out3 = v3(out, Dh)
        if gain is None:
            P.tt('dve', out3, v3(src, Dh), bc_last(rs[:npart, :H], Dh), ALU.mult)
        else:
            t = P.tmp('rmst', [128, 384], F32, 1)
            P.tt('dve', v3(t[:npart, :n], Dh), v3(src, Dh), bc_last(rs[:npart, :H], Dh), ALU.mult)
            P.tt('dve', out3, v3(t[:npart, :n], Dh), bc_mid(gain, H), ALU.mult)
        return rs

    def rms_stats(self, srcs, Dh=None):
        P = self.P
        srcs = [(x[0], x[1], x[2] if len(x) > 2 else Dh) for x in srcs]
        Hs = [src.shape[1] // d for src, _, d in srcs]
        Htot = sum(Hs)
        ss = P.tmp('ss16', [128, 16], F32, 2)
        h0 = 0
        for (src, scr, d), H in zip(srcs, Hs):
            P.act(scr, src, AF.Square, scale=float(d) ** -0.5)
            P.reduce(ss[:, h0:h0 + H], v3(scr, d))
            h0 += H
        ms = P.tmp('ms16', [128, 16], F32, 2)
        P.ts('dve', ms[:, :Htot], ss[:, :Htot], EPS, None, ALU.add)
        rs = P.tmp('rs16', [128, 16], F32, 2)
        P.tt('pool', rs[:, :Htot], ms[:, :Htot], self.neghalf[:, :Htot], ALU.pow)
        return rs

    def rms_apply(self, src, rs, H, Dh, gain, out, out3=None, tname='rmst'):
        P = self.P
        n = H * Dh
        if out3 is None:
            out3 = v3(out, Dh)
        if gain is None:
            P.tt('dve', out3, v3(src, Dh), bc_last(rs, Dh), ALU.mult)
        else:
            t = P.tmp(tname, [128, 384], F32, 1)
            P.tt('dve', v3(t[:, :n], Dh), v3(src, Dh), bc_last(rs, Dh), ALU.mult)
            P.tt('dve', out3, v3(t[:, :n], Dh), bc_mid(gain, H), ALU.mult)

    def rope(self, src3, cos, sin, out3, npart=128):
        P = self.P
        H = src3.shape[1]
        half = src3.shape[2] // 2
        x1 = src3[:, :, 0:half]
        x2 = src3[:, :, half:2 * half]
        cb = bc_mid(cos, H)
        sb_ = bc_mid(sin, H)
        ta = P.tmp('ropea', [128, 128], F32, 1)[:npart, :H * half].rearrange("p (h d) -> p h d", d=half)
        tb = P.tmp('ropeb', [128, 128], F32, 1)[:npart, :H * half].rearrange("p (h d) -> p h d", d=half)
        tc = P.tmp('ropec', [128, 128], F32, 1)[:npart, :H * half].rearrange("p (h d) -> p h d", d=half)
        td = P.tmp('roped', [128, 128], F32, 1)[:npart, :H * half].rearrange("p (h d) -> p h d", d=half)
        e2 = 'dve' if 'PSUM' in str(src3.space) else 'pool'
        P.tt('dve', ta, x1, cb, ALU.mult)
        P.tt(e2, tb, x2, sb_, ALU.mult)
        P.tt('dve', out3[:, :, 0:half], ta, tb, ALU.subtract)
        P.tt('dve', tc, x2, cb, ALU.mult)
        P.tt(e2, td, x1, sb_, ALU.mult)
        P.tt('dve', out3[:, :, half:2 * half], tc, td, ALU.add)

    def transposes(self, blocks, dst, npart=128, eng='act'):
        P = self.P
        pt = self.psum_next('tr')
        w = blocks[0].shape[1]
        for i, b in enumerate(blocks):
            P.tr(pt[:w, i * 128:i * 128 + npart], b, self.ident_bf[:npart, :npart])
        n = len(blocks)
        if n == 1:
            P.copy(eng, dst, pt[:w, 0:npart])
        else:
            P.copy(eng, dst, pt[:w, 0:n * 128].rearrange("p (c t) -> p c t", t=128)[:, :, 0:npart])

    def silu(self, zsrc, n, out):
        P = self.P
        th = P.tmp('silu_th', [128, 256], BF, 1)
        zh = P.tmp('silu_zh', [128, 256], F32, 1)
        P.act(th[:, :n], zsrc, AF.Tanh, scale=0.5)
        P.act(zh[:, :n], zsrc, AF.Copy, scale=0.5)
        P.stt(out, th[:, :n], 1.0, zh[:, :n], ALU.add, ALU.mult)

    def attention(self, qT, keys, scale, oacc, fp32=False):
        P = self.P
        groups = [keys[i:i + 4] for i in range(0, len(keys), 4)]
        nkeys = len(keys)
        state = {'k': 0}

        def emit_pv(grp, pt):
            for j, kd in enumerate(grp):
                nk = kd['nk']
                P.mm(oacc, pt[:nk, j * 128:(j + 1) * 128], kd['v'],
                     start=(state['k'] == 0), stop=(state['k'] == nkeys - 1))
                state['k'] += 1

        pend = None
        for grp in groups:
            sbk = self.psum_next('S')
            for j, kd in enumerate(grp):
                nk = kd['nk']
                o = sbk[:nk, j * 128:(j + 1) * 128]
                nb = len(kd['biases'])
                P.mm(o, kd['kT'], qT, start=True, stop=(nb == 0))
                for bi, (bl, br) in enumerate(kd['biases']):
                    P.mm(o, bl, br, start=False, stop=(bi == nb - 1))
            if fp32:
                pt = P.tmp('PTf', [128, 128], F32, 2)
            else:
                pt = P.tmp('PT', [128, 512], BF, 2)
            if all(kd['nk'] == 128 for kd in grp):
                n = len(grp) * 128
                P.act(pt[:, :n], sbk[:, :n], AF.Exp, scale=scale)
            else:
                for j, kd in enumerate(grp):
                    nk = kd['nk']
                    P.act(pt[:nk, j * 128:(j + 1) * 128], sbk[:nk, j * 128:(j + 1) * 128], AF.Exp, scale=scale)
            if pend is not None:
                emit_pv(*pend)
            pend = (grp, pt)
        emit_pv(*pend)

    def attention_batch(self, jobs, look=None):
        P = self.P
        if look is None:
            look = self.look
        flat = []
        for ji, (qT, keys, scale, oacc) in enumerate(jobs):
            n = len(keys)
            for i, kd in enumerate(keys):
                flat.append((ji, kd, i, n))
        groups = [flat[i:i + 4] for i in range(0, len(flat), 4)]
        scale = jobs[0][2]

        def emit_s(grp):
            sbk = self.psum_next('S')
            for j, (ji, kd, i, n) in enumerate(grp):
                qT = jobs[ji][0]
                o = sbk[:, j * 128:(j + 1) * 128]
                nb = len(kd['biases'])
                P.mm(o, kd['kT'], qT, start=True, stop=(nb == 0))
                for bi, (bl, br) in enumerate(kd['biases']):
                    P.mm(o, bl, br, start=False, stop=(bi == nb - 1))
            pt = P.tmp('PT', [128, 512], BF, 3)
            nn = len(grp) * 128
            P.act(pt[:, :nn], sbk[:, :nn], AF.Exp, scale=scale)
            return pt

        def emit_pv(grp, pt):
            for j, (ji, kd, i, n) in enumerate(grp):
                oacc = jobs[ji][3]
                P.mm(oacc, pt[:, j * 128:(j + 1) * 128], kd['v'], start=(i == 0), stop=(i == n - 1))

        pend = []
        for g in groups:
            pt = emit_s(g)
            pend.append((g, pt))
            if len(pend) > look:
                emit_pv(*pend.pop(0))
        while pend:
            emit_pv(*pend.pop(0))

    def trig_table(self, out, ang, shape, is_cos):
        P = self.P
        n = 1
        for s in shape:
            n *= s
        bigt = P.tmp('big', [128, 1024], F32, 1)
        u = bigt[:, 512:512 + n]
        ki = self.AR[:, 0:1024].bitcast(I32)[:, :n]
        kf = self.AR[:, 1024:2048].bitcast(F32)[:, :n]
        f = u
        m = kf
        a2 = ang if len(shape) == 1 else ang.rearrange("p a b -> p (a b)")
        o2 = out if len(shape) == 1 else out.rearrange("p a b -> p (a b)")
        P.ts('dve', u, a2, 1.0 / TWO_PI, 0.25 if is_cos else 0.0, ALU.mult, ALU.add)
        P.copy('dve', ki, u)
        P.copy('dve', kf, ki)
        P.tt('dve', f, u, kf, ALU.subtract)
        P.ts('dve', m, f, 0.5, None, ALU.is_gt)
        P.tt('dve', f, f, m, ALU.subtract)
        P.ts('dve', m, f, -0.5, None, ALU.is_lt)
        P.tt('dve', f, f, m, ALU.add)
        P.act(o2, f, AF.Sin, scale=TWO_PI * (1.0 - 2e-6))

    def build(self):
        nc, P = self.nc, self.P
        L = DEPTH
        dt_in = {}

        def din(name, shape):
            dt_in[name] = nc.dram_tensor(name, list(shape), F32, kind="ExternalInput").ap()
            return dt_in[name]

        x = din("x", [S, D])
        mem = din("mem", [256, D])
        norm_gain = din("norm_gain", [L, D])
        w_in = din("w_in", [L, D, 3116])
        w_out = din("w_out", [L, D, D])
        nsa_qk_gain = din("nsa_qk_gain", [L, 4, 64])
        nsa_cmp_pe = din("nsa_cmp_pe", [L, 2, 32, 64])
        nsa_w_cmp = din("nsa_w_cmp", [L, 2, 2048, 64])
        diff_qk_gain = din("diff_qk_gain", [L, 2, 32])
        diff_lambda = din("diff_lambda", [L, 4, 32])
        diff_subln_gain = din("diff_subln_gain", [L, 64])
        mla_cq_gain = din("mla_cq_gain", [L, 256])
        mla_ckv_gain = din("mla_ckv_gain", [L, 128])
        mla_w_uq = din("mla_w_uq", [L, 256, 384])
        mla_w_ukv = din("mla_w_ukv", [L, 128, 512])
        mla_qk_gain = din("mla_qk_gain", [L, 2, 96])
        mem_norm_gain = din("mem_norm_gain", [L, D])
        mem_w_kv = din("mem_w_kv", [L, D, 512])
        mem_qk_gain = din("mem_qk_gain", [L, 2, 64])
        out = nc.dram_tensor("out", [S, D], F32, kind="ExternalOutput").ap()

        self.psum_u = [P.ps(f"ps_u{i}", [128, 512], F32) for i in range(4)]
        self.psum_uctr = 0
        self.psum_pools['S'] = [P.ps(f"ps_s{i}", [128, 512], F32) for i in range(2)]
        self.psum_pools['O'] = [P.ps(f"ps_o{i}", [128, 512], F32) for i in range(2)]
        for k in self.psum_pools:
            self.psum_ctr[k] = 0
        self.look = 1
        all_S = list(self.psum_pools['S'])
        all_O = list(self.psum_pools['O'])

        def set_pools(deep):
            if deep:
                self.psum_pools['S'] = all_S + [all_O[1]]
                self.psum_pools['O'] = [all_O[0]]
                self.look = 2
            else:
                self.psum_pools['S'] = list(all_S)
                self.psum_pools['O'] = list(all_O)
                self.look = 1
            self.psum_ctr['S'] = 0
            self.psum_ctr['O'] = 0

        X = P.sb("X", [128, NT, D], F32)
        WIN = P.sb("WIN", [128, 8, 1024], BF)
        WOUT = P.sb("WOUT", [128, 2, 1024], BF)
        AR = P.sb("arena", [128, 12352], BF)
        self.AR = AR
        hT = P.sb("hT", [128, 8, S], BF)
        rstd_all = P.sb("rstd_all", [128, NT], F32)
        ss_all = P.sb("ss_all", [128, NT], F32)

        self.ident_bf = P.sb("ident_bf", [128, 128], BF)
        ident_f = P.sb("ident_f", [32, 32], F32)
        caus_b = P.sb("caus_b", [128, 128], BF)
        win_b = P.sb("win_b", [128, 128], BF)
        cmp_b = P.sb("cmp_b", [128, 2048], BF)
        Eexp = P.sb("Eexp", [128, 2048], BF)
        Cmat = P.sb("Cmat", [128, 32], F32)
        Ctmp = P.sb("Ctmp", [128, 32], F32)
        selA = P.sb("selA", [128, 8, 32], F32)
        selB = P.sb("selB", [128, 8, 32], F32)
        self.neghalf = P.sb("neghalf", [128, NT], F32)
        cos32 = P.sb("cos32", [128, NT, 32], F32)
        sin32 = P.sb("sin32", [128, NT, 32], F32)
        cos16 = P.sb("cos16", [128, NT, 16], F32)
        sin16 = P.sb("sin16", [128, NT, 16], F32)
        cosC = P.sb("cosC", [128, 32], F32)
        sinC = P.sb("sinC", [128, 32], F32)

        P.memset('dve', self.neghalf, -0.5)
        P.memset('pool', self.ident_bf, 1.0)
        P.asel(self.ident_bf, self.ident_bf, [[-1, 128]], ALU.is_equal, 0.0, 0, 1)
        P.memset('pool', ident_f, 1.0)
        P.asel(ident_f, ident_f, [[-1, 32]], ALU.is_equal, 0.0, 0, 1)
        P.memset('pool', caus_b, 0.0)
        P.asel(caus_b, caus_b, [[1, 128]], ALU.is_ge, NEGB, 0, -1)
        P.memset('pool', win_b, 0.0)
        P.asel(win_b, win_b, [[-1, 128]], ALU.is_gt, NEGB, 0, 1)
        P.memset('pool', cmp_b, 0.0)
        P.asel(cmp_b, cmp_b, [[1, 2048]], ALU.is_ge, NEGB, -31, -16)
        P.memset('pool', Eexp, 1.0)
        P.asel(Eexp, Eexp, [[1, 2048]], ALU.is_ge, 0.0, 0, -64)
        P.asel(Eexp, Eexp, [[-1, 2048]], ALU.is_ge, 0.0, 63, 64)
        P.memset('pool', Cmat, 1.0)
        P.memset('pool', Ctmp, 1.0)
        P.asel(Cmat, Cmat, [[-4, 32]], ALU.is_ge, 0.0, 0, 1)
        P.asel(Cmat, Cmat, [[4, 32]], ALU.is_ge, 0.0, 4, -1)
        P.asel(Ctmp, Ctmp, [[-4, 32]], ALU.is_ge, 0.0, -1, 1)
        P.asel(Ctmp, Ctmp, [[4, 32]], ALU.is_ge, 0.0, 3, -1)
        P.tt('pool', Cmat, Cmat, Ctmp, ALU.add)
        P.memset('pool', selA, 1.0)
        P.asel(selA, selA, [[128, 8], [-64, 32]], ALU.is_ge, 0.0, 1024 - 128, 1)
        P.asel(selA, selA, [[0, 8], [1, 32]], ALU.is_ge, 0.0, -1, 0)
        P.ts('pool', selB, selA, -100.0, 100.0, ALU.mult, ALU.add)
        P.asel(selB, selB, [[128, 8], [-64, 32]], ALU.is_ge, -1.0, 1024, 1)

        posf = P.sb("posf", [128, NT], F32)
        P.iota(posf, [[128, NT]], 0, 1)
        idx32 = P.sb("idx32", [128, 32], F32)
        P.iota(idx32, [[1, 32]], 0, 0)
        invf32 = P.sb("invf32", [128, 32], F32)
        invf16 = P.sb("invf16", [128, 16], F32)
        P.act(invf32, idx32, AF.Exp, scale=-math.log(THETA) / 32.0)
        P.act(invf16, idx32[:, 0:16], AF.Exp, scale=-math.log(THETA) / 16.0)
        bigt = P.tmp('big', [128, 1024], F32, 1)
        ang = bigt[:, 0:512].rearrange("p (a b) -> p a b", b=32)
        P.tt('dve', ang, bc_last(posf, 32), bc_mid(invf32, NT), ALU.mult)
        self.trig_table(cos32, ang, [NT, 32], True)
        self.trig_table(sin32, ang, [NT, 32], False)
        ang16 = bigt[:, 0:256].rearrange("p (a b) -> p a b", b=16)
        P.tt('dve', ang16, bc_last(posf, 16), bc_mid(invf16, NT), ALU.mult)
        self.trig_table(cos16, ang16, [NT, 16], True)
        self.trig_table(sin16, ang16, [NT, 16], False)
        posc = P.sb("posc", [128, 1], F32)
        P.iota(posc, [[1, 1]], 31, 16)
        angC = bigt[:, 0:32]
        P.ts('dve', angC, invf32, posc[:, 0:1], None, ALU.mult)
        self.trig_table(cosC, angC, [32], True)
        self.trig_table(sinC, angC, [32], False)
        self.dump("cos32", cos32)
        self.dump("sin16", sin16)
        self.dump("cosC", cosC)
        self.dump("selA", selA)
        self.dump("selB", selB)
        self.dump("Cmat", Cmat)
        self.dump("Eexp", Eexp, BF)
        self.dump("cmp_b", cmp_b, BF)

        g8 = P.sb("g8", [8, 128], F32)
        normgT = P.sb("normgT", [128, 8], F32)
        memgT = P.sb("memgT", [128, 8], F32)
        nsa_g = P.sb("nsa_g", [128, 256], F32)
        diff_g = P.sb("diff_g", [128, 64], F32)
        lam = P.sb("lam", [128, 128], F32)
        subln_g = P.sb("subln_g", [128, 64], F32)
        cq_g = P.sb("cq_g", [128, 256], F32)
        ckv_g = P.sb("ckv_g", [128, 128], F32)
        mla_g = P.sb("mla_g", [128, 192], F32)
        memqk_g = P.sb("memqk_g", [128, 128], F32)
        Wc = P.tmp('big', [128, 1024], F32, 1).bitcast(BF).rearrange("p (l e) -> p l e", e=64)
        pe32 = P.sb("pe32", [32, 128], F32)
        peT2 = P.sb("peT2", [128, 32], F32)
        Wuq = P.sb("Wuq", [128, 2, 384], BF)
        Wukv = P.sb("Wukv", [128, 512], BF)
        neg_lam = P.sb("neg_lam", [128, 1], F32)
        kcmpT = P.sb("kcmpT", [64, 128], BF)
        VC = P.sb("VC", [128, 97], BF)
        P.memset('pool', VC, 1.0)
        P.copy('pool', VC[:, 65:97], Cmat)

        A_kT64 = AR[0:64, 0:8192].rearrange("p (h t) -> p h t", t=S)
        A_kT96 = AR[:, 0:8192].rearrange("p (h t) -> p h t", t=S)
        A_V4 = AR[:, 8192:12352].rearrange("p (t h c) -> p t h c", h=4, c=65)
        A_kswT = AR[0:64, 0:4096].rearrange("p (b t) -> p b t", t=S)
        A_kcvcT = AR[:, 4096:6144]
        A_VSW = AR[:, 6144:8224].rearrange("p (t b c) -> p t b c", b=2, c=65)
        A_kcp = AR[:, 8224:12320].rearrange("p (l n) -> p l n", n=128)
        A_kTm = AR[0:64, 0:1024].rearrange("p (h t) -> p h t", t=256)
        A_Vm = AR[:, 1024:1544].rearrange("p (t h c) -> p t h c", h=4, c=65)
        A_memT = AR[:, 2048:4096].rearrange("p (c t) -> p c t", t=256)

        for t in range(NT):
            P.dma('sp', X[:, t, :], x[t * 128:(t + 1) * 128, :])

        APc = type(AR)

        for l in range(self.nlayers):
            lam_init = 0.8 - 0.6 * math.exp(-0.3 * l)
            P.dma('sp', g8, norm_gain[l].rearrange("(c p) -> c p", p=128))
            pf = self.psum_next('proj')
            P.tr(pf[:, 0:8], g8, ident_f[0:8, 0:8])
            P.copy('dve', normgT, pf[:, 0:8])
            P.dma('sp', g8, mem_norm_gain[l].rearrange("(c p) -> c p", p=128))
            pf = self.psum_next('proj')
            P.tr(pf[:, 0:8], g8, ident_f[0:8, 0:8])
            P.copy('dve', memgT, pf[:, 0:8])
            P.dma('sp', nsa_g, nsa_qk_gain[l].rearrange("a d -> (a d)").partition_broadcast(128))
            P.dma('sp', diff_g, diff_qk_gain[l].rearrange("a d -> (a d)").partition_broadcast(128))
            P.dma('sp', lam, diff_lambda[l].rearrange("a d -> (a d)").partition_broadcast(128))
            P.dma('sp', subln_g, diff_subln_gain[l].partition_broadcast(128))
            P.dma('sp', cq_g, mla_cq_gain[l].partition_broadcast(128))
            P.dma('sp', ckv_g, mla_ckv_gain[l].partition_broadcast(128))
            P.dma('sp', mla_g, mla_qk_gain[l].rearrange("a d -> (a d)").partition_broadcast(128))
            P.dma('sp', memqk_g, mem_qk_gain[l].rearrange("a d -> (a d)").partition_broadcast(128))
            P.dma('sp', pe32.rearrange("l (k d) -> l k d", k=2), nsa_cmp_pe[l].rearrange("k l d -> l k d"))
            pf = self.psum_next('proj')
            P.tr(pf[:, 0:32], pe32, ident_f[0:32, 0:32])
            P.copy('dve', peT2, pf[:, 0:32])
            P.dma('pool', Wuq, mla_w_uq[l].rearrange("(c p) n -> p c n", p=128))
            P.dma('pool', Wukv, mla_w_ukv[l])
            lt = P.tmp('lamt', [128, 64], F32, 1)
            ls = P.tmp('lams', [128, 2], F32, 1)
            P.tt('dve', lt[:, 0:32], lam[:, 0:32], lam[:, 32:64], ALU.mult)
            P.tt('dve', lt[:, 32:64], lam[:, 64:96], lam[:, 96:128], ALU.mult)
            P.reduce(ls, v3(lt, 32))
            le = P.tmp('lame', [128, 2], F32, 1)
            P.act(le, ls, AF.Exp)
            P.tt('dve', neg_lam, le[:, 1:2], le[:, 0:1], ALU.subtract)
            P.ts('dve', neg_lam, neg_lam, -lam_init, None, ALU.add)
            sublns = P.tmp('sublns', [128, 64], F32, 1)
            P.ts('dve', sublns, subln_g, 1.0 - lam_init, None, ALU.mult)

            for t in range(NT):
                junk = P.tmp('big', [128, 1024], F32, 1)
                P.act(junk, X[:, t, :], AF.Square, accum=ss_all[:, t:t + 1])
            ms_all = P.tmp('ms_all', [128, NT], F32, 1)
            P.ts('dve', ms_all, ss_all, 1.0 / D, EPS, ALU.mult, ALU.add)
            P.tt('pool', rstd_all, ms_all, self.neghalf, ALU.pow)
            nsa_pre = (0 in self.groups)
            if nsa_pre and self.win_loaded != (l, 0):
                P.dma('pool', WIN[:, :, 0:908], w_in[l].rearrange("(c p) n -> p c n", p=128)[:, :, 0:908])
                self.win_loaded = (l, 0)
            for t in range(NT):
                hb = P.tmp('hb', [128, 1024], BF, 1)
                P.act(hb, X[:, t, :], AF.Copy, scale=rstd_all[:, t:t + 1])
                pt = self.psum_next('tr')
                for c in range(8):
                    P.tr(pt[:, c * 128:(c + 1) * 128], hb[:, c * 128:(c + 1) * 128], self.ident_bf)
                P.tt('dve', hT[:, :, t * 128:(t + 1) * 128], pt.rearrange("p (c t) -> p c t", t=128),
                     bc_last(normgT, 128), ALU.mult)
                if nsa_pre and t % 4 == 3:
                    tc = t // 4
                    pa = self.psum_next('proj')
                    for c in range(8):
                        P.mm(pa, WIN[:, c, 256:384], hT[:, c, tc * 512:(tc + 1) * 512], start=(c == 0), stop=(c == 7))
                    P.copy('act', A_kcvcT[:, tc * 512:(tc + 1) * 512], pa)

            def make_hT(t):
                return hT[:, :, t * 128:(t + 1) * 128]

            def out_proj_a(c):
                y_bf = c['yb']
                yT = P.tmp('yT', [128, 2, 128], BF, 2)
                self.transposes([y_bf[:, 0:128], y_bf[:, 128:256]], yT, eng='act')
                c['yT'] = yT

            def out_proj(t, c, last_group):
                yT = c['yT']
                for nb in range(2):
                    pb = self.psum_next('proj')
                    for cc in range(2):
                        P.mm(pb, yT[:, cc, :], WOUT[:, cc, nb * 512:(nb + 1) * 512], start=(cc == 0), stop=(cc == 1))
                    P.tt('dve', X[:, t, nb * 512:(nb + 1) * 512], pb, X[:, t, nb * 512:(nb + 1) * 512], ALU.add)
                if last_group and l == self.nlayers - 1:
                    P.dma('sp', out[t * 128:(t + 1) * 128, :], X[:, t, :])

            def proj(dst, ncols, c0, hTt):
                for c in range(8):
                    P.mm(dst[:, 0:ncols], hTt[:, c, :], WIN[:, c, c0:c0 + ncols],
                         start=(c == 0), stop=(c == 7))

            GW = {0: (0, 908), 1: (908, 1024), 2: (1932, 672), 3: (2604, 512)}

            def load_win(ll, g):
                c0, ncols = GW[g]
                P.dma('pool', WIN[:, :, 0:ncols], w_in[ll].rearrange("(c p) n -> p c n", p=128)[:, :, c0:c0 + ncols])
                if g == 3:
                    P.dma('pool', WIN[:, :, 512:1024], mem_w_kv[ll].rearrange("(c p) n -> p c n", p=128))

            def load_group_w(c0, ncols, g):
                if not (self.win_loaded == (l, g)):
                    load_win(l, g)
                    self.win_loaded = (l, g)
                P.dma('pool', WOUT, w_out[l, 256 * g:256 * g + 256, :].rearrange("(c p) n -> p c n", p=128))

            def prefetch_next(g):
                gi = list(self.groups).index(g)
                if gi + 1 < len(self.groups):
                    nxt = (l, self.groups[gi + 1])
                elif l + 1 < self.nlayers:
                    nxt = (l + 1, self.groups[0])
                else:
                    return
                load_win(*nxt)
                self.win_loaded = nxt

            lastg = self.groups[-1]

            def run_pipeline(A, B, B2, C1, C2, g):
                ctxs = {0: A(0, make_hT(0), None)}
                if B is not None:
                    B(0, ctxs[0])
                for t in range(NT):
                    done = []

                    def hook(t=t, done=done):
                        B2(t, ctxs[t])
                        done.append(1)
                    if t + 1 < NT:
                        ctxs[t + 1] = A(t + 1, make_hT(t + 1), hook)
                        if t + 1 == NT - 1:
                            prefetch_next(g)
                    if not done:
                        hook()
                    tail_done = []

                    def tail(t=t, tail_done=tail_done):
                        if tail_done:
                            return
                        tail_done.append(1)
                        if t >= 1:
                            out_proj_a(ctxs[t - 1])
                        if t >= 2:
                            C2(t - 2, ctxs[t - 2])
                            del ctxs[t - 2]
                    mid = None
                    if B is not None:
                        if t + 1 < NT:
                            def mid(t=t):
                                B(t + 1, ctxs[t + 1])
                    else:
                        mid = tail
                    C1(t, ctxs[t], mid)
                    tail()
                C2(NT - 2, ctxs[NT - 2])
                out_proj_a(ctxs[NT - 1])
                C2(NT - 1, ctxs[NT - 1])

            def causal_keys(kT_of, v_of, t, t0=0, extra=None):
                keys = []
                for kt in range(t0, t + 1):
                    b = []
                    if extra is not None:
                        b += extra(kt)
                    if kt == t:
                        b.append((self.ident_bf, caus_b))
                    keys.append(dict(kT=kT_of(kt), v=v_of(kt), nk=128, biases=b))
                return keys

            if 0 in self.groups:
                set_pools(False)
                load_group_w(0, 908, 0)
                for k in range(2):
                    P.dma('pool', Wc[64 * k:64 * k + 64, :, :], nsa_w_cmp[l, k].rearrange("(l d) e -> d l e", d=64))
                P.memset('pool', A_VSW, 1.0)
                if l == 0:
                    selbs = [P.sb(f"selb{i}", [128, 128], BF) for i in range(2)]
                    for sb_t in selbs:
                        P.memset('pool', sb_t, 0.0)
                win_ap = APc(tensor=A_kcvcT.tensor, offset=A_kcvcT.offset,
                             ap=[list(A_kcvcT.ap[0]), [1, 32], [16, 127]])
                P.memset('pool', A_kcp[:, :, 127:128], 0.0)
                P.tt('dve', A_kcp[:, :, 0:127], win_ap, bc_last(peT2, 127), ALU.add)
                pc = self.psum_next('proj')
                pcv = self.psum_next('proj')
                for l_ in range(32):
                    P.mm(pc[:, 0:64], A_kcp[0:64, l_, :], Wc[0:64, l_, :], start=(l_ == 0), stop=(l_ == 31))
                for l_ in range(32):
                    P.mm(pcv[:, 64:128], A_kcp[64:128, l_, :], Wc[64:128, l_, :], start=(l_ == 0), stop=(l_ == 31))
                kcn = P.tmp('kcn', [128, 64], F32, 1)
                self.rms(pc[:, 0:64], 1, 64, nsa_g[:, 64:128], kcn)
                kcb = P.tmp('kcb', [128, 64], BF, 1)
                self.rope(v3(kcn, 64), cosC, sinC, v3(kcb, 64))
                self.transposes([kcb], kcmpT)
                P.copy('dve', VC[:, 0:64], pcv[:, 64:128])

                def nsaA(t, hTt, hook=None):
                    pa = self.psum_next('proj')
                    proj(pa, 256, 0, hTt)
                    pb = self.psum_next('proj')
                    proj(pb, 268, 384, hTt)
                    P.copy('act', A_VSW[:, t, :, 0:64], v3(pb[:, 0:256], 64)[:, 1:4:2, :])
                    gth = P.tmp('gth', [128, 12], F32)
                    P.act(gth, pb[:, 256:268], AF.Tanh, scale=0.5)
                    gt = P.tmp('gt', [128, 12], F32)
                    P.ts('dve', gt, gth, 0.5, 0.5, ALU.mult, ALU.add)
                    sq_a = P.tmp('sq', [128, 384], F32, 1)
                    sq_b = P.tmp('rmst', [128, 384], F32, 1)
                    rs = self.rms_stats([(pa[:, 0:256], sq_a[:, 0:256]), (pb[:, 0:256], sq_b[:, 0:256])], 64)
                    qn = P.tmp('f384', [128, 384], F32, 2)[:, 0:256]
                    self.rms_apply(pa[:, 0:256], rs[:, 0:4], 4, 64, nsa_g[:, 0:64], qn)
                    qb = P.tmp('b384', [128, 384], BF, 2)[:, 0:256]
                    self.rope(v3(qn, 64), cos32[:, t, :], sin32[:, t, :], v3(qb, 64))
                    kn = P.tmp('f384', [128, 384], F32, 2)[:, 0:256]
                    self.rms_apply(pb[:, 0:256], rs[:, 4:8], 4, 64, None, kn)
                    kn3 = v3(kn, 64)
                    kg = P.tmp('rmst', [128, 384], F32, 1)[:, 0:128]
                    P.tt('dve', v3(kg, 64), kn3[:, 0:4:2, :], v3(nsa_g[:, 128:256], 64), ALU.mult)
                    kb = P.tmp('kb', [128, 128], BF)
                    self.rope(v3(kg, 64), cos32[:, t, :], sin32[:, t, :], v3(kb, 64))
                    if hook is not None:
                        hook()
                    pz = self.psum_next('proj')
                    proj(pz, 256, 652, hTt)
                    zs = P.tmp('zs', [128, 256], F32, 2)
                    self.silu(pz[:, 0:256], 256, zs)
                    return dict(qb=qb, kb=kb, gt=gt, zs=zs)

                def nsaB(t, c):
                    tsl = slice(t * 128, (t + 1) * 128)
                    qT = P.tmp('qTt', [128, 8, 128], BF, 1)[0:64, 0:4, :]
                    pt = self.psum_next('tr')
                    blocks = [c['qb'][:, h * 64:(h + 1) * 64] for h in range(4)] + [c['kb'][:, 0:64], c['kb'][:, 64:128]]
                    for i, b in enumerate(blocks):
                        P.tr(pt[:64, i * 128:(i + 1) * 128], b, self.ident_bf)
                    P.copy('act', qT, pt[:64, 0:512].rearrange("p (c t) -> p c t", t=128))
                    P.copy('act', A_kswT[:, :, tsl], pt[:64, 512:768].rearrange("p (c t) -> p c t", t=128))
                    c['qT'] = qT

                def nsaC(t, c, mid=None):
                    tsl = slice(t * 128, (t + 1) * 128)
                    qT, gt, zs = c['qT'], c['gt'], c['zs']
                    OA = self.psum_next('O')
                    jobs = []
                    for h in range(4):
                        keys = [dict(kT=kcmpT, v=VC, nk=128, biases=[(self.ident_bf, cmp_b[:, tsl])])]
                        jobs.append((qT[:, h, :], keys, 0.125, OA[:, h * 97:(h + 1) * 97]))
                    OC = self.psum_next('O')
                    wext = lambda kt: ([(self.ident_bf, win_b)] if kt == t - 4 else [])
                    for h in range(4):
                        keys = causal_keys(lambda kt: A_kswT[:, 1, kt * 128:(kt + 1) * 128],
                                           lambda kt: A_VSW[:, kt, 1, :], t, max(0, t - 4), wext)
                        jobs.append((qT[:, h, :], keys, 0.125, OC[:, h * 65:(h + 1) * 65]))
                    self.attention_batch(jobs)
                    OA3 = OA[:, 0:388].rearrange("p (h c) -> p h c", c=97)
                    dc = P.tmp('dc', [128, 4], F32)
                    P.ts('dve', dc, OA3[:, :, 64], 1e-30, None, ALU.max)
                    rc = P.tmp('rc', [128, 4], F32)
                    P.recip(rc, dc)
                    selbT = None
                    if t >= 8:
                        psl = P.tmp('psl', [128, 32], F32)
                        P.ts('dve', psl, OA3[:, 0, 65:97], rc[:, 0:1], None, ALU.mult)
                        for h in range(1, 4):
                            psl2 = P.tmp('psl', [128, 32], F32)
                            P.stt(psl2, OA3[:, h, 65:97], rc[:, h:h + 1], psl, ALU.mult, ALU.add)
                            psl = psl2
                        sc = P.tmp('sc', [128, 32], F32, 1)
                        P.tt('dve', sc, psl, selA[:, t - 8, :], ALU.mult)
                        sc1 = P.tmp('sc1', [128, 32], F32, 1)
                        P.tt('dve', sc1, sc, selB[:, t - 8, :], ALU.add)
                        cmpt = P.tmp('big', [128, 1024], F32, 1)
                        cmp3 = cmpt.rearrange("p (j i) -> p j i", i=32)
                        P.tt('dve', cmp3, bc_mid(sc1, 32), bc_last(sc1, 32), ALU.is_gt)
                        rank = P.tmp('rank', [128, 32], F32, 1)
                        P.reduce(rank, cmp3)
                        sel = P.tmp('sel', [128, 32], F32, 1)
                        P.ts('dve', sel, rank, 15.5, None, ALU.is_lt)
                        selb = selbs[t % 2]
                        P.ts('dve', selb[:, 0:32], sel, -NEGB, NEGB, ALU.mult, ALU.add)
                        selbT = P.tmp('selbT', [128, 128], BF, 1)
                    fc = P.tmp('fc', [128, 4], F32)
                    P.tt('dve', fc, rc, gt[:, 0:4], ALU.mult)
                    acc = P.tmp('acc', [128, 256], F32)
                    P.tt('dve', v3(acc, 64), OA3[:, :, 0:64], bc_last(fc, 64), ALU.mult)
                    OC3 = OC[:, 0:260].rearrange("p (h c) -> p h c", c=65)
                    rw = P.tmp('rsl', [128, 4], F32)
                    P.recip(rw, OC3[:, :, 64])
                    fw_ = P.tmp('fs', [128, 4], F32)
                    P.tt('dve', fw_, rw, gt[:, 8:12], ALU.mult)
                    t3 = P.tmp('t2', [128, 256], F32)
                    P.tt('dve', v3(t3, 64), OC3[:, :, 0:64], bc_last(fw_, 64), ALU.mult)
                    acc3 = P.tmp('acc', [128, 256], F32)
                    P.tt('pool', acc3, acc, t3, ALU.add)
                    if selbT is not None:
                        if mid is not None:
                            mid()
                        self.transposes([selb], selbT)
                    OB = self.psum_next('O')
                    ext = (lambda kt: [(Eexp[:, kt * 128:(kt + 1) * 128], selbT)]) if selbT is not None else None
                    jobs = []
                    for h in range(4):
                        keys = causal_keys(lambda kt: A_kswT[:, 0, kt * 128:(kt + 1) * 128],
                                           lambda kt: A_VSW[:, kt, 0, :], t, 0, ext)
                        jobs.append((qT[:, h, :], keys, 0.125, OB[:, h * 65:(h + 1) * 65]))
                    self.attention_batch(jobs)
                    OB3 = OB[:, 0:260].rearrange("p (h c) -> p h c", c=65)
                    rs_ = P.tmp('rsl', [128, 4], F32)
                    P.recip(rs_, OB3[:, :, 64])
                    fs = P.tmp('fs', [128, 4], F32)
                    P.tt('dve', fs, rs_, gt[:, 4:8], ALU.mult)
                    t2 = P.tmp('t2', [128, 256], F32)
                    P.tt('dve', v3(t2, 64), OB3[:, :, 0:64], bc_last(fs, 64), ALU.mult)
                    acc2 = P.tmp('acc', [128, 256], F32)
                    P.tt('pool', acc2, acc3, t2, ALU.add)
                    yb = P.tmp('yb', [128, 256], BF, 2)
                    P.tt('dve', yb, acc2, zs, ALU.mult)
                    if l == 0:
                        self.dump(f"y0_{t}", yb, BF)
                    c['yb'] = yb

                run_pipeline(nsaA, None, nsaB, nsaC, lambda t, c: out_proj(t, c, lastg == 0), 0)

            if 1 in self.groups:
                set_pools(False)
                load_group_w(908, 1024, 1)
                P.memset('pool', A_V4, 1.0)
                if l == 0:
                    qpads = [P.sb(f"qpad{i}", [128, 8, 64], BF) for i in range(2)]
                    for qp in qpads:
                        P.memset('pool', qp, 0.0)

                def dfA(t, hTt, hook=None):
                    pa = self.psum_next('proj')
                    proj(pa, 512, 0, hTt)
                    sq_a = P.tmp('sq', [128, 384], F32, 1)
                    sq_b = P.tmp('rmst', [128, 384], F32, 1)
                    rs = self.rms_stats([(pa[:, 0:256], sq_a[:, 0:256]), (pa[:, 256:512], sq_b[:, 0:256])], 32)
                    qn = P.tmp('f384', [128, 384], F32, 2)[:, 0:256]
                    self.rms_apply(pa[:, 0:256], rs[:, 0:8], 8, 32, diff_g[:, 0:32], qn)
                    qr = P.tmp('b384', [128, 384], BF, 2)[:, 0:256]
                    self.rope(v3(qn, 32), cos16[:, t, :], sin16[:, t, :], v3(qr, 32))
                    qp = qpads[t % 2]
                    qp4 = qp.rearrange("p (h i) d -> p h i d", i=2)
                    qr4 = qr.rearrange("p (h i d) -> p h i d", i=2, d=32)
                    P.copy('pool', qp4[:, :, 0, 0:32], qr4[:, :, 0, :])
                    P.copy('pool', qp4[:, :, 1, 32:64], qr4[:, :, 1, :])
                    kn = P.tmp('f384', [128, 384], F32, 2)[:, 0:256]
                    self.rms_apply(pa[:, 256:512], rs[:, 8:16], 8, 32, diff_g[:, 32:64], kn)
                    kb = P.tmp('kbd', [128, 256], BF, 2)
                    self.rope(v3(kn, 32), cos16[:, t, :], sin16[:, t, :], v3(kb, 32))
                    if hook is not None:
                        hook()
                    pb = self.psum_next('proj')
                    proj(pb, 512, 512, hTt)
                    zs = P.tmp('zs', [128, 256], F32, 2)
                    self.silu(pb[:, 256:512], 256, zs)
                    P.copy('act', A_V4[:, t, :, 0:64], v3(pb[:, 0:256], 64))
                    return dict(qp=qp, kb=kb, zs=zs)

                def dfB(t, c):
                    tsl = slice(t * 128, (t + 1) * 128)
                    qT = P.tmp('qTt', [128, 8, 128], BF, 1)[0:64, :, :]
                    self.transposes([c['qp'][:, m, :] for m in range(8)], qT)
                    self.transposes([c['kb'][:, h * 64:(h + 1) * 64] for h in range(4)], A_kT64[:, :, tsl])
                    c['qT'] = qT

                def dfC(t, c, mid=None):
                    qT, zs = c['qT'], c['zs']
                    Os = [self.psum_next('O'), self.psum_next('O')]
                    jobs = []
                    for h in range(4):
                        for i in range(2):
                            keys = causal_keys(lambda kt, h=h: A_kT64[:, h, kt * 128:(kt + 1) * 128],
                                               lambda kt, h=h: A_V4[:, kt, h, :], t)
                            jobs.append((qT[:, 2 * h + i, :], keys, 32 ** -0.5, Os[i][:, h * 65:(h + 1) * 65]))
                    self.attention_batch(jobs)
                    O13 = Os[0][:, 0:260].rearrange("p (h c) -> p h c", c=65)
                    O23 = Os[1][:, 0:260].rearrange("p (h c) -> p h c", c=65)
                    r1 = P.tmp('rsl', [128, 4], F32)
                    P.recip(r1, O13[:, :, 64])
                    r2 = P.tmp('rsl2', [128, 4], F32)
                    P.recip(r2, O23[:, :, 64])
                    r2l = P.tmp('fs', [128, 4], F32)
                    P.ts('dve', r2l, r2, neg_lam[:, 0:1], None, ALU.mult)
                    o1 = P.tmp('acc', [128, 256], F32)
                    P.tt('dve', v3(o1, 64), O13[:, :, 0:64], bc_last(r1, 64), ALU.mult)
                    o2 = P.tmp('t2', [128, 256], F32)
                    P.tt('dve', v3(o2, 64), O23[:, :, 0:64], bc_last(r2l, 64), ALU.mult)
                    od = P.tmp('acc', [128, 256], F32)
                    P.tt('pool', od, o1, o2, ALU.add)
                    on = P.tmp('t2', [128, 256], F32)
                    self.rms(od, 4, 64, sublns, on)
                    yb = P.tmp('yb', [128, 256], BF, 2)
                    P.tt('dve', yb, on, zs, ALU.mult)
                    if l == 0:
                        self.dump(f"y1_{t}", yb, BF)
                    c['yb'] = yb

                run_pipeline(dfA, None, dfB, dfC, lambda t, c: out_proj(t, c, lastg == 1), 1)

            if 2 in self.groups:
                set_pools(True)
                load_group_w(1932, 672, 2)
                P.memset('pool', A_V4, 1.0)
                if l == 0:
                    bpads = [P.sb(f"bpad{i}", [128, 512], BF) for i in range(2)]
                    for bp in bpads:
                        P.memset('pool', bp, 0.0)

                def mlA(t, hTt, hook=None):
                    pa = self.psum_next('proj')
                    proj(pa, 416, 0, hTt)
                    sq_a = P.tmp('sq', [128, 384], F32, 1)
                    rs = self.rms_stats([(pa[:, 0:256], sq_a[:, 0:256], 256), (pa[:, 256:384], sq_a[:, 256:384], 128)])
                    cqb = P.tmp('b384', [128, 384], BF, 2)[:, 0:256]
                    self.rms_apply(pa[:, 0:256], rs[:, 0:1], 1, 256, cq_g, cqb)
                    ckvb = P.tmp('ckvb', [128, 128], BF, 1)
                    self.rms_apply(pa[:, 256:384], rs[:, 1:2], 1, 128, ckv_g, ckvb)
                    krr = P.tmp('krr', [128, 32], F32, 1)
                    self.rope(v3(pa[:, 384:416], 32), cos16[:, t, :], sin16[:, t, :], v3(krr, 32))
                    if hook is not None:
                        hook()
                    pb = self.psum_next('proj')
                    proj(pb, 256, 416, hTt)
                    zs = P.tmp('zs', [128, 256], F32, 2)
                    self.silu(pb[:, 0:256], 256, zs)
                    return dict(cqb=cqb, ckvb=ckvb, krr=krr, zs=zs)

                def mlB(t, c):
                    cqb, ckvb, krr = c['cqb'], c['ckvb'], c['krr']
                    cqT = P.tmp('cqT', [128, 2, 128], BF, 1)
                    self.transposes([cqb[:, 0:128], cqb[:, 128:256]], cqT)
                    ckvT = P.tmp('ckvT', [128, 128], BF, 1)
                    self.transposes([ckvb], ckvT)
                    pq = self.psum_next('proj')
                    for cc in range(2):
                        P.mm(pq[:, 0:384], cqT[:, cc, :], Wuq[:, cc, :], start=(cc == 0), stop=(cc == 1))
                    pq3 = pq[:, 0:384].rearrange("p (h c) -> p h c", c=96)
                    qc = P.tmp('f384', [128, 384], F32, 2)
                    qc3 = v3(qc, 96)
                    P.copy('act', qc3[:, :, 0:64], pq3[:, :, 0:64])
                    self.rope(pq3[:, :, 64:96], cos16[:, t, :], sin16[:, t, :], qc3[:, :, 64:96])
                    pkv = self.psum_next('proj')
                    P.mm(pkv, ckvT, Wukv, start=True, stop=True)
                    pkv3 = v3(pkv, 128)
                    kc_ = P.tmp('f384', [128, 384], F32, 2)
                    kc3 = v3(kc_, 96)
                    P.copy('act', kc3[:, :, 0:64], pkv3[:, :, 0:64])
                    P.copy('dve', kc3[:, :, 64:96], bc_mid(krr, 4))
                    P.copy('act', A_V4[:, t, :, 0:64], pkv3[:, :, 64:128])
                    sq_a = P.tmp('sq', [128, 384], F32, 1)
                    sq_b = P.tmp('rmst', [128, 384], F32, 1)
                    rs = self.rms_stats([(qc, sq_a), (kc_, sq_b)], 96)
                    qb = bpads[0]
                    self.rms_apply(qc, rs[:, 0:4], 4, 96, mla_g[:, 0:96], None, out3=v3(qb, 128)[:, :, 0:96], tname='sq')
                    kb = bpads[1]
                    self.rms_apply(kc_, rs[:, 4:8], 4, 96, mla_g[:, 96:192], None, out3=v3(kb, 128)[:, :, 0:96])

                def mlB2(t, c):
                    tsl = slice(t * 128, (t + 1) * 128)
                    qb, kb = bpads[0], bpads[1]
                    qT = P.tmp('qTt', [128, 8, 128], BF, 1)[:, 0:4, :]
                    self.transposes([qb[:, h * 128:(h + 1) * 128] for h in range(4)], qT)
                    self.transposes([kb[:, h * 128:(h + 1) * 128] for h in range(4)], A_kT96[:, :, tsl])
                    c['qT'] = qT

                def mlC(t, c, mid=None):
                    qT, zs = c['qT'], c['zs']
                    OA = self.psum_next('O')
                    jobs = []
                    for h in range(4):
                        keys = causal_keys(lambda kt, h=h: A_kT96[:, h, kt * 128:(kt + 1) * 128],
                                           lambda kt, h=h: A_V4[:, kt, h, :], t)
                        jobs.append((qT[:, h, :], keys, 96 ** -0.5, OA[:, h * 65:(h + 1) * 65]))
                    self.attention_batch(jobs[:2])
                    if mid is not None:
                        mid()
                    self.attention_batch(jobs[2:])
                    OA3 = OA[:, 0:260].rearrange("p (h c) -> p h c", c=65)
                    r1 = P.tmp('rsl', [128, 4], F32)
                    P.recip(r1, OA3[:, :, 64])
                    o1 = P.tmp('acc', [128, 256], F32)
                    P.tt('dve', v3(o1, 64), OA3[:, :, 0:64], bc_last(r1, 64), ALU.mult)
                    yb = P.tmp('yb', [128, 256], BF, 2)
                    P.tt('dve', yb, o1, zs, ALU.mult)
                    if l == 0:
                        self.dump(f"y2_{t}", yb, BF)
                    c['yb'] = yb

                run_pipeline(mlA, mlB, mlB2, mlC, lambda t, c: out_proj(t, c, lastg == 2), 2)

            if 3 in self.groups:
                set_pools(True)
                load_group_w(2604, 512, 3)
                P.memset('pool', A_Vm, 1.0)
                for mt in range(2):
                    mx = P.tmp('big', [128, 1024], F32, 1)
                    P.dma('sp', mx, mem[mt * 128:(mt + 1) * 128, :])
                    junk = P.tmp('qTt', [128, 8, 128], BF, 1).rearrange("p a b -> p (a b)")
                    ss = P.tmp('ss', [128, 8], F32, 3)
                    P.act(junk, mx, AF.Square, accum=ss[:, 0:1])
                    ms = P.tmp('ms', [128, 8], F32, 3)
                    P.ts('dve', ms[:, 0:1], ss[:, 0:1], 1.0 / D, EPS, ALU.mult, ALU.add)
                    rs = P.tmp('rs', [128, 8], F32, 3)
                    P.tt('pool', rs[:, 0:1], ms[:, 0:1], self.neghalf[:, 0:1], ALU.pow)
                    hb = P.tmp('hb', [128, 1024], BF, 1)
                    P.ts('dve', hb, mx, rs[:, 0:1], None, ALU.mult)
                    pt = self.psum_next('tr')
                    for c in range(8):
                        P.tr(pt[:, c * 128:(c + 1) * 128], hb[:, c * 128:(c + 1) * 128], self.ident_bf)
                    P.tt('dve', A_memT[:, :, mt * 128:(mt + 1) * 128], pt.rearrange("p (c t) -> p c t", t=128),
                         bc_last(memgT, 128), ALU.mult)
                for mt in range(2):
                    pk = self.psum_next('proj')
                    for c in range(8):
                        P.mm(pk, A_memT[:, c, mt * 128:(mt + 1) * 128], WIN[:, c, 512:1024], start=(c == 0), stop=(c == 7))
                    kb = P.tmp('b384', [128, 384], BF, 2)[:, 0:256]
                    self.rms(pk[:, 0:256], 4, 64, memqk_g[:, 64:128], kb)
                    self.transposes([kb[:, h * 64:(h + 1) * 64] for h in range(4)], A_kTm[:, :, mt * 128:(mt + 1) * 128])
                    P.copy('act', A_Vm[:, mt, :, 0:64], v3(pk[:, 256:512], 64))

                def mmA(t, hTt, hook=None):
                    pa = self.psum_next('proj')
                    proj(pa, 256, 0, hTt)
                    qb = P.tmp('b384', [128, 384], BF, 2)[:, 0:256]
                    self.rms(pa[:, 0:256], 4, 64, memqk_g[:, 0:64], qb)
                    if hook is not None:
                        hook()
                    pz = self.psum_next('proj')
                    proj(pz, 256, 256, hTt)
                    zs = P.tmp('zs', [128, 256], F32, 2)
                    self.silu(pz[:, 0:256], 256, zs)
                    return dict(qb=qb, zs=zs)

                def mmB(t, c):
                    qT = P.tmp('qTt', [128, 8, 128], BF, 1)[0:64, 0:4, :]
                    self.transposes([c['qb'][:, h * 64:(h + 1) * 64] for h in range(4)], qT)
                    c['qT'] = qT

                def mmC(t, c, mid=None):
                    qT, zs = c['qT'], c['zs']
                    OA = self.psum_next('O')
                    jobs = []
                    for h in range(4):
                        keys = [dict(kT=A_kTm[:, h, kt * 128:(kt + 1) * 128], v=A_Vm[:, kt, h, :], nk=128, biases=[])
                                for kt in range(2)]
                        jobs.append((qT[:, h, :], keys, 0.125, OA[:, h * 65:(h + 1) * 65]))
                    self.attention_batch(jobs)
                    OA3 = OA[:, 0:260].rearrange("p (h c) -> p h c", c=65)
                    r1 = P.tmp('rsl', [128, 4], F32)
                    P.recip(r1, OA3[:, :, 64])
                    o1 = P.tmp('acc', [128, 256], F32)
                    P.tt('dve', v3(o1, 64), OA3[:, :, 0:64], bc_last(r1, 64), ALU.mult)
                    yb = P.tmp('yb', [128, 256], BF, 2)
                    P.tt('dve', yb, o1, zs, ALU.mult)
                    if l == 0:
                        self.dump(f"y3_{t}", yb, BF)
                    c['yb'] = yb

                run_pipeline(mmA, None, mmB, mmC, lambda t, c: out_proj(t, c, lastg == 3), 3)

        P.finalize()
        P.close()
        return nc


_INPUT_NAMES = ["x", "mem", "norm_gain", "w_in", "w_out", "nsa_qk_gain", "nsa_cmp_pe", "nsa_w_cmp",
                "diff_qk_gain", "diff_lambda", "diff_subln_gain", "mla_cq_gain", "mla_ckv_gain", "mla_w_uq",
                "mla_w_ukv", "mla_qk_gain", "mem_norm_gain", "mem_w_kv", "mem_qk_gain"]


def run_model(inputs, debug=None, nlayers=DEPTH, groups=(0, 1, 2, 3), ncores=8):
    nc = bass.Bass("TRN2", target_bir_lowering=False)
    m = Model(nc, debug=debug, nlayers=nlayers, groups=groups)
    m.build()
    arrs = {k: np.ascontiguousarray(np.asarray(inputs[k], dtype=np.float32)) for k in _INPUT_NAMES}
    in_maps = []
    for b in range(ncores):
        d = {k: arrs[k] for k in _INPUT_NAMES if k not in ("x", "mem")}
        d["x"] = np.ascontiguousarray(arrs["x"][b])
        d["mem"] = np.ascontiguousarray(arrs["mem"][b])
        in_maps.append(d)
    res = run_bass_kernel_spmd(nc, in_maps, core_ids=list(range(ncores)))
    return res, m


def kernel(**inputs):
    res, m = run_model(inputs)
    outs = [np.asarray(res.results[b]["out"], dtype=np.float32) for b in range(8)]
    return np.stack(outs, axis=0)
```

```python
import numpy as np
import concourse.bass as bass
import concourse.mybir as mybir
from concourse.bass_utils import run_bass_kernel_spmd

F32 = mybir.dt.float32
BF = mybir.dt.bfloat16
I32 = mybir.dt.int32
ALU = mybir.AluOpType
AF = mybir.ActivationFunctionType
AX = mybir.AxisListType

ENGS = ('pe', 'act', 'dve', 'pool', 'sp')
EPOCH = 3000


class Op:
    __slots__ = ('eng', 'fn', 'deps', 'is_dma', 'idx', 'gidx', 'need_inc', 'sem', 'val',
                 'waits', 'vc', 'dma_prev')

    def __init__(self, eng, fn, is_dma):
        self.eng = eng
        self.fn = fn
        self.is_dma = is_dma
        self.deps = {}
        self.need_inc = False
        self.sem = None
        self.val = 0
        self.waits = []
        self.vc = None
        self.dma_prev = None


def _bbox(ap):
    sp = str(ap.space)
    if 'SB' not in sp and 'PSUM' not in sp:
        return None
    pat = ap.ap
    pstride, npart = pat[0]
    off = ap.offset
    if pstride == 0:
        ts = ap.tensor.shape
        row = 1
        for s in ts[1:]:
            row *= s
        pstride_eff = row
    else:
        pstride_eff = pstride
    p0 = off // pstride_eff
    f0 = off % pstride_eff
    ext = 0
    for st, cnt in pat[1:]:
        ext += abs(st) * (cnt - 1)
    return (ap.tensor.name, 'PSUM' in sp, p0, p0 + npart, f0, f0 + ext + 1)


class Prog:
    def __init__(self, nc):
        self.nc = nc
        self.ops = {e: [] for e in ENGS}
        self.all_ops = []
        self.track = {}
        self.tmp_pool = {}
        self.tmp_ctr = {}
        self._stack = []

    def sb(self, name, shape, dtype):
        cm = self.nc.sbuf_tensor(name, list(shape), dtype)
        t = cm.__enter__()
        self._stack.append(cm)
        return t[:]

    def ps(self, name, shape, dtype):
        cm = self.nc.psum_tensor(name, list(shape), dtype)
        t = cm.__enter__()
        self._stack.append(cm)
        return t[:]

    def tmp(self, name, shape, dtype, nbuf=2):
        key = name
        if key not in self.tmp_pool:
            self.tmp_pool[key] = [self.sb(f"{name}_{i}", shape, dtype) for i in range(nbuf)]
            self.tmp_ctr[key] = 0
        i = self.tmp_ctr[key]
        self.tmp_ctr[key] = (i + 1) % len(self.tmp_pool[key])
        return self.tmp_pool[key][i]

    def op(self, eng, fn, reads=(), writes=(), dma=False):
        o = Op(eng, fn, dma)
        o.gidx = len(self.all_ops)
        for ap in reads:
            self._access(o, ap, 'r')
        for ap in writes:
            self._access(o, ap, 'w')
        o.idx = len(self.ops[eng])
        self.ops[eng].append(o)
        self.all_ops.append(o)
        return o

    def _access(self, o, ap, kind):
        bb = _bbox(ap)
        if bb is None:
            return
        name, is_psum, p0, p1, f0, f1 = bb
        lst = self.track.setdefault(name, [])
        if is_psum:
            for e in lst:
                d = e[4]
                if d is o:
                    continue
                if d.eng != o.eng or d.is_dma or o.is_dma:
                    o.deps[d] = o.deps.get(d, False) or (kind == 'r' and e[5] == 'w')
                elif kind == 'r' and e[5] == 'w' and not (p1 <= e[0] or e[1] <= p0 or f1 <= e[2] or e[3] <= f0):
                    o.deps[d] = True
            lst[:] = [e for e in lst if not (e[4].eng == o.eng and e[5] == kind and not e[4].is_dma)]
            lst.append([p0, p1, f0, f1, o, kind])
            return
        keep = []
        for e in lst:
            d = e[4]
            ov = not (p1 <= e[0] or e[1] <= p0 or f1 <= e[2] or e[3] <= f0)
            if ov and d is not o:
                if kind == 'r':
                    if e[5] == 'w':
                        o.deps[d] = True
                else:
                    o.deps[d] = o.deps.get(d, False)
            if kind == 'w' and e[0] >= p0 and e[1] <= p1 and e[2] >= f0 and e[3] <= f1 and d is not o:
                continue
            if kind == 'r' and e[5] == 'r' and d.eng == o.eng and not d.is_dma and not o.is_dma \
                    and e[0] == p0 and e[1] == p1 and e[2] == f0 and e[3] == f1:
                continue
            keep.append(e)
        keep.append([p0, p1, f0, f1, o, kind])
        lst[:] = keep

    def mm(self, out, lhsT, rhs, start=True, stop=True):
        return self.op('pe', lambda e: e.matmul(out, lhsT, rhs, start=start, stop=stop),
                       reads=[lhsT, rhs], writes=[out])

    def tr(self, out, in_, ident):
        return self.op('pe', lambda e: e.transpose(out, in_, ident), reads=[in_, ident], writes=[out])

    def act(self, out, in_, func, scale=1.0, bias=0.0, accum=None):
        rd = [in_]
        if not isinstance(scale, (int, float)):
            rd.append(scale)
        if not isinstance(bias, (int, float)):
            rd.append(bias)
        wr = [out] + ([accum] if accum is not None else [])
        kw = {}
        if accum is not None:
            kw['accum_out'] = accum
        return self.op('act', lambda e: e.activation(out, in_, func, bias=bias, scale=scale, **kw),
                       reads=rd, writes=wr)

    def _e(self, eng):
        return eng

    def tt(self, eng, out, in0, in1, op):
        return self.op(eng, lambda e: e.tensor_tensor(out, in0, in1, op), reads=[in0, in1], writes=[out])

    def ts(self, eng, out, in0, s1, s2, op0, op1=None):
        rd = [in0]
        if not isinstance(s1, (int, float)):
            rd.append(s1)
        if s2 is not None and not isinstance(s2, (int, float)):
            rd.append(s2)
        if op1 is None:
            return self.op(eng, lambda e: e.tensor_scalar(out, in0, s1, None, op0), reads=rd, writes=[out])
        return self.op(eng, lambda e: e.tensor_scalar(out, in0, s1, s2, op0, op1), reads=rd, writes=[out])

    def stt(self, out, in0, scalar, in1, op0, op1):
        rd = [in0, in1]
        if not isinstance(scalar, (int, float)):
            rd.append(scalar)
        return self.op('dve', lambda e: e.scalar_tensor_tensor(out, in0, scalar, in1, op0, op1),
                       reads=rd, writes=[out])

    def copy(self, eng, out, in_):
        if eng == 'act':
            return self.op('act', lambda e: e.copy(out, in_), reads=[in_], writes=[out])
        return self.op(eng, lambda e: e.tensor_copy(out, in_), reads=[in_], writes=[out])

    def reduce(self, out, in_, op=None, axis=None):
        op = op or ALU.add
        axis = axis or AX.X
        return self.op('dve', lambda e: e.tensor_reduce(out, in_, axis, op), reads=[in_], writes=[out])

    def recip(self, out, in_):
        return self.op('dve', lambda e: e.reciprocal(out, in_), reads=[in_], writes=[out])

    def memset(self, eng, ap, val):
        return self.op(eng, lambda e: e.memset(ap, val), writes=[ap])

    def iota(self, out, pattern, base, cm):
        return self.op('pool', lambda e: e.iota(out, pattern, base=base, channel_multiplier=cm,
                                                 allow_small_or_imprecise_dtypes=True), writes=[out])

    def asel(self, out, in_, pattern, cmp, fill, base, cm):
        return self.op('pool', lambda e: e.affine_select(out, in_, pattern, cmp, fill, base=base,
                                                          channel_multiplier=cm), reads=[in_], writes=[out])

    def max8(self, out, in_):
        return self.op('dve', lambda e: e.max(out, in_), reads=[in_], writes=[out])

    def match_replace(self, out, to_rep, vals, imm):
        return self.op('dve', lambda e: e.match_replace(out, to_rep, vals, imm), reads=[to_rep, vals], writes=[out])

    def dma(self, q, out, in_, **kw):
        return self.op(q, lambda e: e.dma_start(out=out, in_=in_, **kw), reads=[in_], writes=[out], dma=True)

    def finalize(self):
        nc = self.nc
        eidx = {e: i for i, e in enumerate(ENGS)}
        NDMA = 24
        known = {e: [-1] * len(ENGS) for e in ENGS}
        dma_known = {e: set() for e in ENGS}
        dma_rr = {e: 0 for e in ENGS}
        dma_last = {}
        for o in self.all_ops:
            E = o.eng
            kn = known[E]
            dk = dma_known[E]
            waits = []
            for d, raw in sorted(o.deps.items(), key=lambda kv: kv[0].gidx):
                if d.is_dma:
                    if d in dk:
                        continue
                    waits.append(d)
                    dk.add(d)
                    if d.vc is not None:
                        for i in range(len(ENGS)):
                            if d.vc[i] > kn[i]:
                                kn[i] = d.vc[i]
                    continue
                if d.eng == E and not o.is_dma:
                    if E == 'pe':
                        continue
                    if kn[eidx[E]] >= d.idx:
                        continue
                    waits.append(d)
                    kn[eidx[E]] = d.idx
                    continue
                if kn[eidx[d.eng]] >= d.idx:
                    continue
                waits.append(d)
                for i in range(len(ENGS)):
                    if d.vc[i] > kn[i]:
                        kn[i] = d.vc[i]
                if d.idx > kn[eidx[d.eng]]:
                    kn[eidx[d.eng]] = d.idx
            for d in waits:
                d.need_inc = True
            o.waits = waits
            if o.is_dma:
                slot = dma_rr[E]
                dma_rr[E] = (slot + 1) % NDMA
                o.sem = (E, slot)
                prev = dma_last.get((E, slot))
                o.dma_prev = prev
                dma_last[(E, slot)] = o
                o.need_inc = True
                o.vc = list(kn)
            else:
                vc = list(kn)
                vc[eidx[E]] = o.idx
                o.vc = vc
        sems = {}

        def get_sem(key):
            if key not in sems:
                cm = nc.semaphore(f"s_{key[0]}_{key[1]}")
                sems[key] = cm.__enter__()
                self._stack.append(cm)
            return sems[key]

        cnt = {e: 0 for e in ENGS}
        dma_tot = {}
        for o in self.all_ops:
            if o.is_dma:
                key = ('d' + o.sem[0], o.sem[1])
                dma_tot[key] = dma_tot.get(key, 0) + 16
                o.sem = key
                o.val = dma_tot[key]
            elif o.need_inc:
                c = cnt[o.eng]
                o.sem = (o.eng, c // EPOCH)
                o.val = c % EPOCH + 1
                cnt[o.eng] = c + 1
        self.nsems = None
        engmap = {'pe': 'tensor', 'act': 'scalar', 'dve': 'vector', 'pool': 'gpsimd', 'sp': 'sync'}
        for o in self.all_ops:
            if o.sem is not None:
                get_sem(o.sem)
        prog = self

        def emit(engname):
            def f(eng):
                for o in prog.ops[engname]:
                    for d in o.waits:
                        eng.wait_ge(sems[d.sem], d.val)
                    if o.is_dma and o.dma_prev is not None:
                        eng.wait_ge(sems[o.sem], o.dma_prev.val)
                    ins = o.fn(eng)
                    if o.is_dma:
                        ins.then_inc(sems[o.sem], 16)
                    elif o.need_inc:
                        ins.then_inc(sems[o.sem], 1)
                last = {}
                for o in prog.ops[engname]:
                    if o.is_dma:
                        last[o.sem] = o.val
                for k, v in last.items():
                    eng.wait_ge(sems[k], v)
            return f

        with nc.Block() as block:
            block.tensor(emit('pe'))
            block.scalar(emit('act'))
            block.vector(emit('dve'))
            block.gpsimd(emit('pool'))
            block.sync(emit('sp'))
        n = {e: len(self.ops[e]) for e in ENGS}
        w = {e: sum(len(o.waits) for o in self.ops[e]) for e in ENGS}
        print("ops per engine", n, "waits", w, "incs", cnt, "nsems", len(sems))

    def close(self):
        while self._stack:
            cm = self._stack.pop()
            cm.__exit__(None, None, None)

import math
import os
NOSEL = os.environ.get('NSA_NOSEL') == '1'
NOCMPKV = os.environ.get('NSA_NOCMPKV') == '1'

S = 2048
D = 1024
NT = 16
DEPTH = 2
EPS = 1e-6
NEGB = -30000.0
THETA = 10000.0
TWO_PI = 2.0 * math.pi


def bc_last(ap2, d):
    s = list(ap2.shape)
    return ap2.unsqueeze(len(s)).to_broadcast(s + [d])


def bc_mid(ap2, h):
    s = list(ap2.shape)
    return ap2.unsqueeze(1).to_broadcast([s[0], h] + s[1:])


def v3(ap, d):
    return ap.rearrange("p (h d) -> p h d", d=d)


class Model:
    def __init__(self, nc, debug=None, nlayers=DEPTH, groups=(0, 1, 2, 3)):
        self.nc = nc
        self.debug = debug or {}
        self.nlayers = nlayers
        self.groups = groups
        self.P = Prog(nc)
        self.dbg_out = {}
        self.psum_pools = {}
        self.psum_ctr = {}
        self.win_loaded = None

    def psum_next(self, pool):
        if pool in ('proj', 'tr'):
            b = self.psum_u[self.psum_uctr]
            self.psum_uctr = (self.psum_uctr + 1) % len(self.psum_u)
            return b.bitcast(BF) if pool == 'tr' else b
        i = self.psum_ctr[pool]
        self.psum_ctr[pool] = (i + 1) % len(self.psum_pools[pool])
        return self.psum_pools[pool][i]

    def dump(self, name, ap, dtype=F32):
        if name not in self.debug:
            return
        shp = list(ap.shape)
        d = self.nc.dram_tensor("dbg_" + name, shp, dtype, kind="ExternalOutput").ap()
        self.P.dma('sp', d, ap)
        self.dbg_out[name] = "dbg_" + name

    def rms(self, src, H, Dh, gain, out, npart=128, out3=None):
        P = self.P
        n = H * Dh
        sq = P.tmp('sq', [128, 384], F32, 1)
        P.act(sq[:npart, :n], src, AF.Square)
        ss = P.tmp('ss', [128, 8], F32, 3)
        P.reduce(ss[:npart, :H], v3(sq[:npart, :n], Dh))
        ms = P.tmp('ms', [128, 8], F32, 3)
        P.ts('dve', ms[:npart, :H], ss[:npart, :H], 1.0 / Dh, EPS, ALU.mult, ALU.add)
        rs = P.tmp('rs', [128, 8], F32, 3)
        P.tt('pool', rs[:npart, :H], ms[:npart, :H], self.neghalf[:npart, :H], ALU.pow)
        if out3 is None:
            out3 = v3(out, Dh)
        if gain is None:
            P.tt('dve', out3, v3(src, Dh), bc_last(rs[:npart, :H], Dh), ALU.mult)
        else:
            t = P.tmp('rmst', [128, 384], F32, 1)
            P.tt('dve', v3(t[:npart, :n], Dh), v3(src, Dh), bc_last(rs[:npart, :H], Dh), ALU.mult)
            P.tt('dve', out3, v3(t[:npart, :n], Dh), bc_mid(gain, H), ALU.mult)
        return rs

    def rms_stats(self, srcs, Dh=None):
        P = self.P
        srcs = [(x[0], x[1], x[2] if len(x) > 2 else Dh) for x in srcs]
        Hs = [src.shape[1] // d for src, _, d in srcs]
        Htot = sum(Hs)
        ss = P.tmp('ss16', [128, 16], F32, 2)
        h0 = 0
        for (src, scr, d), H in zip(srcs, Hs):
            P.act(scr, src, AF.Square, scale=float(d) ** -0.5)
            P.reduce(ss[:, h0:h0 + H], v3(scr, d))
            h0 += H
        ms = P.tmp('ms16', [128, 16], F32, 2)
        P.ts('dve', ms[:, :Htot], ss[:, :Htot], EPS, None, ALU.add)
        rs = P.tmp('rs16', [128, 16], F32, 2)
        P.tt('pool', rs[:, :Htot], ms[:, :Htot], self.neghalf[:, :Htot], ALU.pow)
        return rs

    def rms_apply(self, src, rs, H, Dh, gain, out, out3=None, tname='rmst'):
        P = self.P
        n = H * Dh
        if out3 is None:
            out3 = v3(out, Dh)
        if gain is not None and H == 1 and out is not None:
            P.stt(out, src, rs[:, 0:1], gain, ALU.mult, ALU.mult)
            return
        if gain is None:
            P.tt('dve', out3, v3(src, Dh), bc_last(rs, Dh), ALU.mult)
        else:
            t = P.tmp(tname, [128, 384], F32, 1)
            P.tt('dve', v3(t[:, :n], Dh), v3(src, Dh), bc_last(rs, Dh), ALU.mult)
            P.tt('dve', out3, v3(t[:, :n], Dh), bc_mid(gain, H), ALU.mult)

    def rope(self, src3, cos, sin, out3, npart=128):
        P = self.P
        H = src3.shape[1]
        half = src3.shape[2] // 2
        x1 = src3[:, :, 0:half]
        x2 = src3[:, :, half:2 * half]
        cb = bc_mid(cos, H)
        sb_ = bc_mid(sin, H)
        ta = P.tmp('ropea', [128, 128], F32, 1)[:npart, :H * half].rearrange("p (h d) -> p h d", d=half)
        tb = P.tmp('ropeb', [128, 128], F32, 1)[:npart, :H * half].rearrange("p (h d) -> p h d", d=half)
        tc = P.tmp('ropec', [128, 128], F32, 1)[:npart, :H * half].rearrange("p (h d) -> p h d", d=half)
        td = P.tmp('roped', [128, 128], F32, 1)[:npart, :H * half].rearrange("p (h d) -> p h d", d=half)
        e2 = 'dve' if 'PSUM' in str(src3.space) else 'pool'
        P.tt('dve', ta, x1, cb, ALU.mult)
        P.tt(e2, tb, x2, sb_, ALU.mult)
        P.tt('dve', out3[:, :, 0:half], ta, tb, ALU.subtract)
        P.tt('dve', tc, x2, cb, ALU.mult)
        P.tt(e2, td, x1, sb_, ALU.mult)
        P.tt('dve', out3[:, :, half:2 * half], tc, td, ALU.add)

    def transposes(self, blocks, dst, npart=128, eng='act'):
        P = self.P
        pt = self.psum_next('tr')
        w = blocks[0].shape[1]
        for i, b in enumerate(blocks):
            P.tr(pt[:w, i * 128:i * 128 + npart], b, self.ident_bf[:npart, :npart])
        n = len(blocks)
        if n == 1:
            P.copy(eng, dst, pt[:w, 0:npart])
        else:
            P.copy(eng, dst, pt[:w, 0:n * 128].rearrange("p (c t) -> p c t", t=128)[:, :, 0:npart])

    def silu(self, zsrc, n, out):
        P = self.P
        th = P.tmp('silu_th', [128, 256], BF, 1)
        zh = P.tmp('silu_zh', [128, 256], F32, 1)
        P.act(th[:, :n], zsrc, AF.Tanh, scale=0.5)
        P.act(zh[:, :n], zsrc, AF.Copy, scale=0.5)
        P.stt(out, th[:, :n], 1.0, zh[:, :n], ALU.add, ALU.mult)

    def attention(self, qT, keys, scale, oacc, fp32=False):
        P = self.P
        groups = [keys[i:i + 4] for i in range(0, len(keys), 4)]
        nkeys = len(keys)
        state = {'k': 0}

        def emit_pv(grp, pt):
            for j, kd in enumerate(grp):
                nk = kd['nk']
                P.mm(oacc, pt[:nk, j * 128:(j + 1) * 128], kd['v'],
                     start=(state['k'] == 0), stop=(state['k'] == nkeys - 1))
                state['k'] += 1

        pend = None
        for grp in groups:
            sbk = self.psum_next('S')
            for j, kd in enumerate(grp):
                nk = kd['nk']
                o = sbk[:nk, j * 128:(j + 1) * 128]
                nb = len(kd['biases'])
                P.mm(o, kd['kT'], qT, start=True, stop=(nb == 0))
                for bi, (bl, br) in enumerate(kd['biases']):
                    P.mm(o, bl, br, start=False, stop=(bi == nb - 1))
            if fp32:
                pt = P.tmp('PTf', [128, 128], F32, 2)
            else:
                pt = P.tmp('PT', [128, 512], BF, 2)
            if all(kd['nk'] == 128 for kd in grp):
                n = len(grp) * 128
                P.act(pt[:, :n], sbk[:, :n], AF.Exp, scale=scale)
            else:
                for j, kd in enumerate(grp):
                    nk = kd['nk']
                    P.act(pt[:nk, j * 128:(j + 1) * 128], sbk[:nk, j * 128:(j + 1) * 128], AF.Exp, scale=scale)
            if pend is not None:
                emit_pv(*pend)
            pend = (grp, pt)
        emit_pv(*pend)

    def attention_batch(self, jobs, look=None):
        P = self.P
        if look is None:
            look = self.look
        flat = []
        for ji, (qT, keys, scale, oacc) in enumerate(jobs):
            n = len(keys)
            for i, kd in enumerate(keys):
                flat.append((ji, kd, i, n))
        groups = [flat[i:i + 4] for i in range(0, len(flat), 4)]
        scale = jobs[0][2]

        def emit_s(grp):
            sbk = self.psum_next('S')
            for j, (ji, kd, i, n) in enumerate(grp):
                qT = jobs[ji][0]
                o = sbk[:, j * 128:(j + 1) * 128]
                nb = len(kd['biases'])
                P.mm(o, kd['kT'], qT, start=True, stop=(nb == 0))
                for bi, (bl, br) in enumerate(kd['biases']):
                    P.mm(o, bl, br, start=False, stop=(bi == nb - 1))
            pt = P.tmp('PT', [128, 512], BF, 3)
            nn = len(grp) * 128
            P.act(pt[:, :nn], sbk[:, :nn], AF.Exp, scale=scale)
            return pt

        def emit_pv(grp, pt):
            for j, (ji, kd, i, n) in enumerate(grp):
                oacc = jobs[ji][3]
                P.mm(oacc, pt[:, j * 128:(j + 1) * 128], kd['v'], start=(i == 0), stop=(i == n - 1))

        pend = []
        for g in groups:
            pt = emit_s(g)
            pend.append((g, pt))
            if len(pend) > look:
                emit_pv(*pend.pop(0))
        while pend:
            emit_pv(*pend.pop(0))

    def trig_table(self, out, ang, shape, is_cos):
        P = self.P
        n = 1
        for s in shape:
            n *= s
        bigt = P.tmp('big', [128, 1024], F32, 1)
        u = bigt[:, 512:512 + n]
        ki = self.AR[:, 0:1024].bitcast(I32)[:, :n]
        kf = self.AR[:, 1024:2048].bitcast(F32)[:, :n]
        f = u
        m = kf
        a2 = ang if len(shape) == 1 else ang.rearrange("p a b -> p (a b)")
        o2 = out if len(shape) == 1 else out.rearrange("p a b -> p (a b)")
        P.ts('dve', u, a2, 1.0 / TWO_PI, 0.25 if is_cos else 0.0, ALU.mult, ALU.add)
        P.copy('dve', ki, u)
        P.copy('dve', kf, ki)
        P.tt('dve', f, u, kf, ALU.subtract)
        P.ts('dve', m, f, 0.5, None, ALU.is_gt)
        P.tt('dve', f, f, m, ALU.subtract)
        P.ts('dve', m, f, -0.5, None, ALU.is_lt)
        P.tt('dve', f, f, m, ALU.add)
        P.act(o2, f, AF.Sin, scale=TWO_PI * (1.0 - 2e-6))

    def build(self):
        nc, P = self.nc, self.P
        L = DEPTH
        dt_in = {}

        def din(name, shape):
            dt_in[name] = nc.dram_tensor(name, list(shape), F32, kind="ExternalInput").ap()
            return dt_in[name]

        x = din("x", [S, D])
        mem = din("mem", [256, D])
        norm_gain = din("norm_gain", [L, D])
        w_in = din("w_in", [L, D, 3116])
        w_out = din("w_out", [L, D, D])
        nsa_qk_gain = din("nsa_qk_gain", [L, 4, 64])
        nsa_cmp_pe = din("nsa_cmp_pe", [L, 2, 32, 64])
        nsa_w_cmp = din("nsa_w_cmp", [L, 2, 2048, 64])
        diff_qk_gain = din("diff_qk_gain", [L, 2, 32])
        diff_lambda = din("diff_lambda", [L, 4, 32])
        diff_subln_gain = din("diff_subln_gain", [L, 64])
        mla_cq_gain = din("mla_cq_gain", [L, 256])
        mla_ckv_gain = din("mla_ckv_gain", [L, 128])
        mla_w_uq = din("mla_w_uq", [L, 256, 384])
        mla_w_ukv = din("mla_w_ukv", [L, 128, 512])
        mla_qk_gain = din("mla_qk_gain", [L, 2, 96])
        mem_norm_gain = din("mem_norm_gain", [L, D])
        mem_w_kv = din("mem_w_kv", [L, D, 512])
        mem_qk_gain = din("mem_qk_gain", [L, 2, 64])
        out = nc.dram_tensor("out", [S, D], F32, kind="ExternalOutput").ap()

        self.psum_u = [P.ps(f"ps_u{i}", [128, 512], F32) for i in range(4)]
        self.psum_uctr = 0
        self.psum_pools['S'] = [P.ps(f"ps_s{i}", [128, 512], F32) for i in range(2)]
        self.psum_pools['O'] = [P.ps(f"ps_o{i}", [128, 512], F32) for i in range(2)]
        for k in self.psum_pools:
            self.psum_ctr[k] = 0
        self.look = 1
        all_S = list(self.psum_pools['S'])
        all_O = list(self.psum_pools['O'])

        def set_pools(deep):
            if deep:
                self.psum_pools['S'] = all_S + [all_O[1]]
                self.psum_pools['O'] = [all_O[0]]
                self.look = 2
            else:
                self.psum_pools['S'] = list(all_S)
                self.psum_pools['O'] = list(all_O)
                self.look = 1
            self.psum_ctr['S'] = 0
            self.psum_ctr['O'] = 0

        X = P.sb("X", [128, NT, D], F32)
        WIN = P.sb("WIN", [128, 8, 1024], BF)
        WOUT = P.sb("WOUT", [128, 2, 1024], BF)
        AR = P.sb("arena", [128, 12352], BF)
        self.AR = AR
        hT = P.sb("hT", [128, 8, S], BF)
        rstd_all = P.sb("rstd_all", [128, NT], F32)
        ss_all = P.sb("ss_all", [128, NT], F32)

        self.ident_bf = P.sb("ident_bf", [128, 128], BF)
        ident_f = P.sb("ident_f", [32, 32], F32)
        caus_b = P.sb("caus_b", [128, 128], BF)
        win_b = P.sb("win_b", [128, 128], BF)
        cmp_b = P.sb("cmp_b", [128, 2048], BF)
        Eexp = P.sb("Eexp", [128, 2048], BF)
        Cmat = P.sb("Cmat", [128, 32], F32)
        Ctmp = P.sb("Ctmp", [128, 32], F32)
        selA = P.sb("selA", [128, 8, 32], F32)
        selB = P.sb("selB", [128, 8, 32], F32)
        self.neghalf = P.sb("neghalf", [128, NT], F32)
        cos32 = P.sb("cos32", [128, NT, 32], F32)
        sin32 = P.sb("sin32", [128, NT, 32], F32)
        cos16 = P.sb("cos16", [128, NT, 16], F32)
        sin16 = P.sb("sin16", [128, NT, 16], F32)
        cosC = P.sb("cosC", [128, 32], F32)
        sinC = P.sb("sinC", [128, 32], F32)

        P.memset('dve', self.neghalf, -0.5)
        P.memset('pool', self.ident_bf, 1.0)
        P.asel(self.ident_bf, self.ident_bf, [[-1, 128]], ALU.is_equal, 0.0, 0, 1)
        P.memset('pool', ident_f, 1.0)
        P.asel(ident_f, ident_f, [[-1, 32]], ALU.is_equal, 0.0, 0, 1)
        P.memset('pool', caus_b, 0.0)
        P.asel(caus_b, caus_b, [[1, 128]], ALU.is_ge, NEGB, 0, -1)
        P.memset('pool', win_b, 0.0)
        P.asel(win_b, win_b, [[-1, 128]], ALU.is_gt, NEGB, 0, 1)
        P.memset('pool', cmp_b, 0.0)
        P.asel(cmp_b, cmp_b, [[1, 2048]], ALU.is_ge, NEGB, -31, -16)
        P.memset('pool', Eexp, 1.0)
        P.asel(Eexp, Eexp, [[1, 2048]], ALU.is_ge, 0.0, 0, -64)
        P.asel(Eexp, Eexp, [[-1, 2048]], ALU.is_ge, 0.0, 63, 64)
        P.memset('pool', Cmat, 1.0)
        P.memset('pool', Ctmp, 1.0)
        P.asel(Cmat, Cmat, [[-4, 32]], ALU.is_ge, 0.0, 0, 1)
        P.asel(Cmat, Cmat, [[4, 32]], ALU.is_ge, 0.0, 4, -1)
        P.asel(Ctmp, Ctmp, [[-4, 32]], ALU.is_ge, 0.0, -1, 1)
        P.asel(Ctmp, Ctmp, [[4, 32]], ALU.is_ge, 0.0, 3, -1)
        P.tt('pool', Cmat, Cmat, Ctmp, ALU.add)
        P.memset('pool', selA, 1.0)
        P.asel(selA, selA, [[128, 8], [-64, 32]], ALU.is_ge, 0.0, 1024 - 128, 1)
        P.asel(selA, selA, [[0, 8], [1, 32]], ALU.is_ge, 0.0, -1, 0)
        P.ts('pool', selB, selA, -100.0, 100.0, ALU.mult, ALU.add)
        P.asel(selB, selB, [[128, 8], [-64, 32]], ALU.is_ge, -1.0, 1024, 1)

        posf = P.sb("posf", [128, NT], F32)
        P.iota(posf, [[128, NT]], 0, 1)
        idx32 = P.sb("idx32", [128, 32], F32)
        P.iota(idx32, [[1, 32]], 0, 0)
        invf32 = P.sb("invf32", [128, 32], F32)
        invf16 = P.sb("invf16", [128, 16], F32)
        P.act(invf32, idx32, AF.Exp, scale=-math.log(THETA) / 32.0)
        P.act(invf16, idx32[:, 0:16], AF.Exp, scale=-math.log(THETA) / 16.0)
        bigt = P.tmp('big', [128, 1024], F32, 1)
        ang = bigt[:, 0:512].rearrange("p (a b) -> p a b", b=32)
        P.tt('dve', ang, bc_last(posf, 32), bc_mid(invf32, NT), ALU.mult)
        self.trig_table(cos32, ang, [NT, 32], True)
        self.trig_table(sin32, ang, [NT, 32], False)
        ang16 = bigt[:, 0:256].rearrange("p (a b) -> p a b", b=16)
        P.tt('dve', ang16, bc_last(posf, 16), bc_mid(invf16, NT), ALU.mult)
        self.trig_table(cos16, ang16, [NT, 16], True)
        self.trig_table(sin16, ang16, [NT, 16], False)
        posc = P.sb("posc", [128, 1], F32)
        P.iota(posc, [[1, 1]], 31, 16)
        angC = bigt[:, 0:32]
        P.ts('dve', angC, invf32, posc[:, 0:1], None, ALU.mult)
        self.trig_table(cosC, angC, [32], True)
        self.trig_table(sinC, angC, [32], False)
        self.dump("cos32", cos32)
        self.dump("sin16", sin16)
        self.dump("cosC", cosC)
        self.dump("selA", selA)
        self.dump("selB", selB)
        self.dump("Cmat", Cmat)
        self.dump("Eexp", Eexp, BF)
        self.dump("cmp_b", cmp_b, BF)

        g8 = P.sb("g8", [8, 128], F32)
        normgT = P.sb("normgT", [128, 8], F32)
        memgT = P.sb("memgT", [128, 8], F32)
        nsa_g = P.sb("nsa_g", [128, 256], F32)
        diff_g = P.sb("diff_g", [128, 64], F32)
        lam = P.sb("lam", [128, 128], F32)
        subln_g = P.sb("subln_g", [128, 64], F32)
        cq_g = P.sb("cq_g", [128, 256], F32)
        ckv_g = P.sb("ckv_g", [128, 128], F32)
        mla_g = P.sb("mla_g", [128, 192], F32)
        memqk_g = P.sb("memqk_g", [128, 128], F32)
        Wc = P.tmp('big', [128, 1024], F32, 1).bitcast(BF).rearrange("p (l e) -> p l e", e=64)
        pe32 = P.sb("pe32", [32, 128], F32)
        peT2 = P.sb("peT2", [128, 32], F32)
        Wuq = P.sb("Wuq", [128, 2, 384], BF)
        Wukv = P.sb("Wukv", [128, 512], BF)
        neg_lam = P.sb("neg_lam", [128, 1], F32)
        kcmpT = P.sb("kcmpT", [64, 128], BF)
        VC = P.sb("VC", [128, 97], BF)
        P.memset('pool', VC, 1.0)
        P.copy('pool', VC[:, 65:97], Cmat)

        A_kT64 = AR[0:64, 0:8192].rearrange("p (h t) -> p h t", t=S)
        A_kT96 = AR[:, 0:8192].rearrange("p (h t) -> p h t", t=S)
        A_V4 = AR[:, 8192:12352].rearrange("p (t h c) -> p t h c", h=4, c=65)
        A_kswT = AR[0:64, 0:4096].rearrange("p (b t) -> p b t", t=S)
        A_kcvcT = AR[:, 4096:6144]
        A_VSW = AR[:, 6144:8224].rearrange("p (t b c) -> p t b c", b=2, c=65)
        A_kcp = AR[:, 8224:12320].rearrange("p (l n) -> p l n", n=128)
        A_kTm = AR[0:64, 0:1024].rearrange("p (h t) -> p h t", t=256)
        A_Vm = AR[:, 1024:1544].rearrange("p (t h c) -> p t h c", h=4, c=65)
        A_memT = AR[:, 2048:4096].rearrange("p (c t) -> p c t", t=256)

        for t in range(NT):
            P.dma('sp', X[:, t, :], x[t * 128:(t + 1) * 128, :])

        APc = type(AR)

        for l in range(self.nlayers):
            lam_init = 0.8 - 0.6 * math.exp(-0.3 * l)
            P.dma('sp', g8, norm_gain[l].rearrange("(c p) -> c p", p=128))
            pf = self.psum_next('proj')
            P.tr(pf[:, 0:8], g8, ident_f[0:8, 0:8])
            P.copy('dve', normgT, pf[:, 0:8])
            P.dma('sp', g8, mem_norm_gain[l].rearrange("(c p) -> c p", p=128))
            pf = self.psum_next('proj')
            P.tr(pf[:, 0:8], g8, ident_f[0:8, 0:8])
            P.copy('dve', memgT, pf[:, 0:8])
            P.dma('sp', nsa_g, nsa_qk_gain[l].rearrange("a d -> (a d)").partition_broadcast(128))
            P.dma('sp', diff_g, diff_qk_gain[l].rearrange("a d -> (a d)").partition_broadcast(128))
            P.dma('sp', lam, diff_lambda[l].rearrange("a d -> (a d)").partition_broadcast(128))
            P.dma('sp', subln_g, diff_subln_gain[l].partition_broadcast(128))
            P.dma('sp', cq_g, mla_cq_gain[l].partition_broadcast(128))
            P.dma('sp', ckv_g, mla_ckv_gain[l].partition_broadcast(128))
            P.dma('sp', mla_g, mla_qk_gain[l].rearrange("a d -> (a d)").partition_broadcast(128))
            P.dma('sp', memqk_g, mem_qk_gain[l].rearrange("a d -> (a d)").partition_broadcast(128))
            P.dma('sp', pe32.rearrange("l (k d) -> l k d", k=2), nsa_cmp_pe[l].rearrange("k l d -> l k d"))
            pf = self.psum_next('proj')
            P.tr(pf[:, 0:32], pe32, ident_f[0:32, 0:32])
            P.copy('dve', peT2, pf[:, 0:32])
            P.dma('pool', Wuq, mla_w_uq[l].rearrange("(c p) n -> p c n", p=128))
            P.dma('pool', Wukv, mla_w_ukv[l])
            lt = P.tmp('lamt', [128, 64], F32, 1)
            ls = P.tmp('lams', [128, 2], F32, 1)
            P.tt('dve', lt[:, 0:32], lam[:, 0:32], lam[:, 32:64], ALU.mult)
            P.tt('dve', lt[:, 32:64], lam[:, 64:96], lam[:, 96:128], ALU.mult)
            P.reduce(ls, v3(lt, 32))
            le = P.tmp('lame', [128, 2], F32, 1)
            P.act(le, ls, AF.Exp)
            P.tt('dve', neg_lam, le[:, 1:2], le[:, 0:1], ALU.subtract)
            P.ts('dve', neg_lam, neg_lam, -lam_init, None, ALU.add)
            sublns = P.tmp('sublns', [128, 64], F32, 1)
            P.ts('dve', sublns, subln_g, 1.0 - lam_init, None, ALU.mult)

            for t in range(NT):
                junk = P.tmp('big', [128, 1024], F32, 1)
                P.act(junk, X[:, t, :], AF.Square, accum=ss_all[:, t:t + 1])
            ms_all = P.tmp('ms_all', [128, NT], F32, 1)
            P.ts('dve', ms_all, ss_all, 1.0 / D, EPS, ALU.mult, ALU.add)
            P.tt('pool', rstd_all, ms_all, self.neghalf, ALU.pow)
            for t in range(NT):
                hb = P.tmp('hb', [128, 1024], BF, 1)
                P.act(hb, X[:, t, :], AF.Copy, scale=rstd_all[:, t:t + 1])
                pt = self.psum_next('tr')
                for c in range(8):
                    P.tr(pt[:, c * 128:(c + 1) * 128], hb[:, c * 128:(c + 1) * 128], self.ident_bf)
                P.tt('dve', hT[:, :, t * 128:(t + 1) * 128], pt.rearrange("p (c t) -> p c t", t=128),
                     bc_last(normgT, 128), ALU.mult)

            def make_hT(t):
                return hT[:, :, t * 128:(t + 1) * 128]

            def out_proj_a(c):
                y_bf = c['yb']
                yT = P.tmp('yT', [128, 2, 128], BF, 2)
                self.transposes([y_bf[:, 0:128], y_bf[:, 128:256]], yT, eng='act')
                c['yT'] = yT

            def out_proj(t, c, last_group):
                yT = c['yT']
                for nb in range(2):
                    pb = self.psum_next('proj')
                    for cc in range(2):
                        P.mm(pb, yT[:, cc, :], WOUT[:, cc, nb * 512:(nb + 1) * 512], start=(cc == 0), stop=(cc == 1))
                    P.tt('dve', X[:, t, nb * 512:(nb + 1) * 512], pb, X[:, t, nb * 512:(nb + 1) * 512], ALU.add)
                if last_group and l == self.nlayers - 1:
                    P.dma('sp', out[t * 128:(t + 1) * 128, :], X[:, t, :])

            def proj(dst, ncols, c0, hTt):
                for c in range(8):
                    P.mm(dst[:, 0:ncols], hTt[:, c, :], WIN[:, c, c0:c0 + ncols],
                         start=(c == 0), stop=(c == 7))

            GW = {0: (0, 908), 1: (908, 1024), 2: (1932, 672), 3: (2604, 512)}

            def load_win(ll, g):
                c0, ncols = GW[g]
                P.dma('pool', WIN[:, :, 0:ncols], w_in[ll].rearrange("(c p) n -> p c n", p=128)[:, :, c0:c0 + ncols])
                if g == 3:
                    P.dma('pool', WIN[:, :, 512:1024], mem_w_kv[ll].rearrange("(c p) n -> p c n", p=128))

            def load_group_w(c0, ncols, g):
                if not (self.win_loaded == (l, g)):
                    load_win(l, g)
                    self.win_loaded = (l, g)
                P.dma('pool', WOUT, w_out[l, 256 * g:256 * g + 256, :].rearrange("(c p) n -> p c n", p=128))

            def prefetch_next(g):
                gi = list(self.groups).index(g)
                if gi + 1 < len(self.groups):
                    nxt = (l, self.groups[gi + 1])
                elif l + 1 < self.nlayers:
                    nxt = (l + 1, self.groups[0])
                else:
                    return
                load_win(*nxt)
                self.win_loaded = nxt

            lastg = self.groups[-1]

            def run_pipeline(A, B, B2, C1, C2, g):
                ctxs = {0: A(0, make_hT(0), None)}
                if B is not None:
                    B(0, ctxs[0])
                for t in range(NT):
                    done = []

                    def hook(t=t, done=done):
                        B2(t, ctxs[t])
                        done.append(1)
                    if t + 1 < NT:
                        ctxs[t + 1] = A(t + 1, make_hT(t + 1), hook)
                        if t + 1 == NT - 1:
                            prefetch_next(g)
                    if not done:
                        hook()
                    tail_done = []

                    def tail(t=t, tail_done=tail_done):
                        if tail_done:
                            return
                        tail_done.append(1)
                        if t >= 1:
                            out_proj_a(ctxs[t - 1])
                        if t >= 2:
                            C2(t - 2, ctxs[t - 2])
                            del ctxs[t - 2]
                    mid = None
                    if B is not None:
                        if t + 1 < NT:
                            def mid(t=t):
                                B(t + 1, ctxs[t + 1])
                    else:
                        mid = tail
                    C1(t, ctxs[t], mid)
                    tail()
                C2(NT - 2, ctxs[NT - 2])
                out_proj_a(ctxs[NT - 1])
                C2(NT - 1, ctxs[NT - 1])

            def causal_keys(kT_of, v_of, t, t0=0, extra=None):
                keys = []
                for kt in range(t0, t + 1):
                    b = []
                    if extra is not None:
                        b += extra(kt)
                    if kt == t:
                        b.append((self.ident_bf, caus_b))
                    keys.append(dict(kT=kT_of(kt), v=v_of(kt), nk=128, biases=b))
                return keys

            if 0 in self.groups:
                set_pools(False)
                load_group_w(0, 908, 0)
                for k in range(2):
                    P.dma('pool', Wc[64 * k:64 * k + 64, :, :], nsa_w_cmp[l, k].rearrange("(l d) e -> d l e", d=64))
                P.memset('pool', A_VSW, 1.0)
                if l == 0:
                    selbs = [P.sb(f"selb{i}", [128, 128], BF) for i in range(2)]
                    for sb_t in selbs:
                        P.memset('pool', sb_t, 0.0)
                for tc in range(4):
                    pa = self.psum_next('proj')
                    for c in range(8):
                        P.mm(pa, WIN[:, c, 256:384], hT[:, c, tc * 512:(tc + 1) * 512], start=(c == 0), stop=(c == 7))
                    P.copy('act', A_kcvcT[:, tc * 512:(tc + 1) * 512], pa)
                win_ap = APc(tensor=A_kcvcT.tensor, offset=A_kcvcT.offset,
                             ap=[list(A_kcvcT.ap[0]), [1, 32], [16, 127]])
                P.memset('pool', A_kcp[:, :, 127:128], 0.0)
                P.tt('dve', A_kcp[:, :, 0:127], win_ap, bc_last(peT2, 127), ALU.add)
                pc = self.psum_next('proj')
                pcv = self.psum_next('proj')
                for l_ in range(32):
                    P.mm(pc[:, 0:64], A_kcp[0:64, l_, :], Wc[0:64, l_, :], start=(l_ == 0), stop=(l_ == 31))
                for l_ in range(32):
                    P.mm(pcv[:, 64:128], A_kcp[64:128, l_, :], Wc[64:128, l_, :], start=(l_ == 0), stop=(l_ == 31))
                kcn = P.tmp('kcn', [128, 64], F32, 1)
                self.rms(pc[:, 0:64], 1, 64, nsa_g[:, 64:128], kcn)
                kcb = P.tmp('kcb', [128, 64], BF, 1)
                self.rope(v3(kcn, 64), cosC, sinC, v3(kcb, 64))
                self.transposes([kcb], kcmpT)
                P.copy('dve', VC[:, 0:64], pcv[:, 64:128])

                def nsaA(t, hTt, hook=None):
                    pa = self.psum_next('proj')
                    proj(pa, 256, 0, hTt)
                    pb = self.psum_next('proj')
                    proj(pb, 268, 384, hTt)
                    P.copy('act', A_VSW[:, t, :, 0:64], v3(pb[:, 0:256], 64)[:, 1:4:2, :])
                    gth = P.tmp('gth', [128, 12], F32)
                    P.act(gth, pb[:, 256:268], AF.Tanh, scale=0.5)
                    gt = P.tmp('gt', [128, 12], F32)
                    P.ts('dve', gt, gth, 0.5, 0.5, ALU.mult, ALU.add)
                    sq_a = P.tmp('sq', [128, 384], F32, 1)
                    sq_b = P.tmp('rmst', [128, 384], F32, 1)
                    rs = self.rms_stats([(pa[:, 0:256], sq_a[:, 0:256]), (pb[:, 0:256], sq_b[:, 0:256])], 64)
                    qn = P.tmp('f384', [128, 384], F32, 2)[:, 0:256]
                    self.rms_apply(pa[:, 0:256], rs[:, 0:4], 4, 64, nsa_g[:, 0:64], qn)
                    qb = P.tmp('b384', [128, 384], BF, 2)[:, 0:256]
                    self.rope(v3(qn, 64), cos32[:, t, :], sin32[:, t, :], v3(qb, 64))
                    kn = P.tmp('f384', [128, 384], F32, 2)[:, 0:256]
                    self.rms_apply(pb[:, 0:256], rs[:, 4:8], 4, 64, None, kn)
                    kn3 = v3(kn, 64)
                    kg = P.tmp('rmst', [128, 384], F32, 1)[:, 0:128]
                    P.tt('dve', v3(kg, 64), kn3[:, 0:4:2, :], v3(nsa_g[:, 128:256], 64), ALU.mult)
                    kb = P.tmp('kb', [128, 128], BF)
                    self.rope(v3(kg, 64), cos32[:, t, :], sin32[:, t, :], v3(kb, 64))
                    if hook is not None:
                        hook()
                    pz = self.psum_next('proj')
                    proj(pz, 256, 652, hTt)
                    zs = P.tmp('zs', [128, 256], F32, 2)
                    self.silu(pz[:, 0:256], 256, zs)
                    return dict(qb=qb, kb=kb, gt=gt, zs=zs)

                def nsaB(t, c):
                    tsl = slice(t * 128, (t + 1) * 128)
                    qT = P.tmp('qTt', [128, 8, 128], BF, 1)[0:64, 0:4, :]
                    pt = self.psum_next('tr')
                    blocks = [c['qb'][:, h * 64:(h + 1) * 64] for h in range(4)] + [c['kb'][:, 0:64], c['kb'][:, 64:128]]
                    for i, b in enumerate(blocks):
                        P.tr(pt[:64, i * 128:(i + 1) * 128], b, self.ident_bf)
                    P.copy('act', qT, pt[:64, 0:512].rearrange("p (c t) -> p c t", t=128))
                    P.copy('act', A_kswT[:, :, tsl], pt[:64, 512:768].rearrange("p (c t) -> p c t", t=128))
                    c['qT'] = qT

                def nsaC(t, c, mid=None):
                    tsl = slice(t * 128, (t + 1) * 128)
                    qT, gt, zs = c['qT'], c['gt'], c['zs']
                    OA = self.psum_next('O')
                    jobs = []
                    for h in range(4):
                        keys = [dict(kT=kcmpT, v=VC, nk=128, biases=[(self.ident_bf, cmp_b[:, tsl])])]
                        jobs.append((qT[:, h, :], keys, 0.125, OA[:, h * 97:(h + 1) * 97]))
                    OC = self.psum_next('O')
                    wext = lambda kt: ([(self.ident_bf, win_b)] if kt == t - 4 else [])
                    for h in range(4):
                        keys = causal_keys(lambda kt: A_kswT[:, 1, kt * 128:(kt + 1) * 128],
                                           lambda kt: A_VSW[:, kt, 1, :], t, max(0, t - 4), wext)
                        jobs.append((qT[:, h, :], keys, 0.125, OC[:, h * 65:(h + 1) * 65]))
                    self.attention_batch(jobs)
                    OA3 = OA[:, 0:388].rearrange("p (h c) -> p h c", c=97)
                    dc = P.tmp('dc', [128, 4], F32)
                    P.ts('dve', dc, OA3[:, :, 64], 1e-30, None, ALU.max)
                    rc = P.tmp('rc', [128, 4], F32)
                    P.recip(rc, dc)
                    selbT = None
                    if t >= 8:
                        psl = P.tmp('psl', [128, 32], F32)
                        P.ts('dve', psl, OA3[:, 0, 65:97], rc[:, 0:1], None, ALU.mult)
                        for h in range(1, 4):
                            psl2 = P.tmp('psl', [128, 32], F32)
                            P.stt(psl2, OA3[:, h, 65:97], rc[:, h:h + 1], psl, ALU.mult, ALU.add)
                            psl = psl2
                        sc = P.tmp('sc', [128, 32], F32, 1)
                        P.tt('dve', sc, psl, selA[:, t - 8, :], ALU.mult)
                        sc1 = P.tmp('sc1', [128, 32], F32, 1)
                        P.tt('dve', sc1, sc, selB[:, t - 8, :], ALU.add)
                        cmpt = P.tmp('big', [128, 1024], F32, 1)
                        cmp3 = cmpt.rearrange("p (j i) -> p j i", i=32)
                        P.tt('dve', cmp3, bc_mid(sc1, 32), bc_last(sc1, 32), ALU.is_gt)
                        rank = P.tmp('rank', [128, 32], F32, 1)
                        P.reduce(rank, cmp3)
                        sel = P.tmp('sel', [128, 32], F32, 1)
                        P.ts('dve', sel, rank, 15.5, None, ALU.is_lt)
                        selb = selbs[t % 2]
                        P.ts('dve', selb[:, 0:32], sel, -NEGB, NEGB, ALU.mult, ALU.add)
                        selbT = P.tmp('selbT', [128, 128], BF, 1)
                    fc = P.tmp('fc', [128, 4], F32)
                    P.tt('dve', fc, rc, gt[:, 0:4], ALU.mult)
                    acc = P.tmp('acc', [128, 256], F32)
                    P.tt('dve', v3(acc, 64), OA3[:, :, 0:64], bc_last(fc, 64), ALU.mult)
                    OC3 = OC[:, 0:260].rearrange("p (h c) -> p h c", c=65)
                    rw = P.tmp('rsl', [128, 4], F32)
                    P.recip(rw, OC3[:, :, 64])
                    fw_ = P.tmp('fs', [128, 4], F32)
                    P.tt('dve', fw_, rw, gt[:, 8:12], ALU.mult)
                    t3 = P.tmp('t2', [128, 256], F32)
                    P.tt('dve', v3(t3, 64), OC3[:, :, 0:64], bc_last(fw_, 64), ALU.mult)
                    acc3 = P.tmp('acc', [128, 256], F32)
                    P.tt('pool', acc3, acc, t3, ALU.add)
                    if selbT is not None:
                        if mid is not None:
                            mid()
                        self.transposes([selb], selbT)
                    OB = self.psum_next('O')
                    ext = (lambda kt: [(Eexp[:, kt * 128:(kt + 1) * 128], selbT)]) if selbT is not None else None
                    jobs = []
                    for h in range(4):
                        keys = causal_keys(lambda kt: A_kswT[:, 0, kt * 128:(kt + 1) * 128],
                                           lambda kt: A_VSW[:, kt, 0, :], t, 0, ext)
                        jobs.append((qT[:, h, :], keys, 0.125, OB[:, h * 65:(h + 1) * 65]))
                    self.attention_batch(jobs)
                    OB3 = OB[:, 0:260].rearrange("p (h c) -> p h c", c=65)
                    rs_ = P.tmp('rsl', [128, 4], F32)
                    P.recip(rs_, OB3[:, :, 64])
                    fs = P.tmp('fs', [128, 4], F32)
                    P.tt('dve', fs, rs_, gt[:, 4:8], ALU.mult)
                    t2 = P.tmp('t2', [128, 256], F32)
                    P.tt('dve', v3(t2, 64), OB3[:, :, 0:64], bc_last(fs, 64), ALU.mult)
                    acc2 = P.tmp('acc', [128, 256], F32)
                    P.tt('pool', acc2, acc3, t2, ALU.add)
                    yb = P.tmp('yb', [128, 256], BF, 2)
                    P.tt('dve', yb, acc2, zs, ALU.mult)
                    if l == 0:
                        self.dump(f"y0_{t}", yb, BF)
                    c['yb'] = yb

                run_pipeline(nsaA, None, nsaB, nsaC, lambda t, c: out_proj(t, c, lastg == 0), 0)

            if 1 in self.groups:
                set_pools(False)
                load_group_w(908, 1024, 1)
                P.memset('pool', A_V4, 1.0)
                if l == 0:
                    qpads = [P.sb(f"qpad{i}", [128, 8, 64], BF) for i in range(2)]
                    for qp in qpads:
                        P.memset('pool', qp, 0.0)

                def dfA(t, hTt, hook=None):
                    pa = self.psum_next('proj')
                    proj(pa, 512, 0, hTt)
                    sq_a = P.tmp('sq', [128, 384], F32, 1)
                    sq_b = P.tmp('rmst', [128, 384], F32, 1)
                    rs = self.rms_stats([(pa[:, 0:256], sq_a[:, 0:256]), (pa[:, 256:512], sq_b[:, 0:256])], 32)
                    qn = P.tmp('f384', [128, 384], F32, 2)[:, 0:256]
                    self.rms_apply(pa[:, 0:256], rs[:, 0:8], 8, 32, diff_g[:, 0:32], qn)
                    qr = P.tmp('b384', [128, 384], BF, 2)[:, 0:256]
                    self.rope(v3(qn, 32), cos16[:, t, :], sin16[:, t, :], v3(qr, 32))
                    qp = qpads[t % 2]
                    qp4 = qp.rearrange("p (h i) d -> p h i d", i=2)
                    qr4 = qr.rearrange("p (h i d) -> p h i d", i=2, d=32)
                    P.copy('pool', qp4[:, :, 0, 0:32], qr4[:, :, 0, :])
                    P.copy('pool', qp4[:, :, 1, 32:64], qr4[:, :, 1, :])
                    kn = P.tmp('f384', [128, 384], F32, 2)[:, 0:256]
                    self.rms_apply(pa[:, 256:512], rs[:, 8:16], 8, 32, diff_g[:, 32:64], kn)
                    kb = P.tmp('kbd', [128, 256], BF, 2)
                    self.rope(v3(kn, 32), cos16[:, t, :], sin16[:, t, :], v3(kb, 32))
                    if hook is not None:
                        hook()
                    pb = self.psum_next('proj')
                    proj(pb, 512, 512, hTt)
                    zs = P.tmp('zs', [128, 256], F32, 2)
                    self.silu(pb[:, 256:512], 256, zs)
                    P.copy('act', A_V4[:, t, :, 0:64], v3(pb[:, 0:256], 64))
                    return dict(qp=qp, kb=kb, zs=zs)

                def dfB(t, c):
                    tsl = slice(t * 128, (t + 1) * 128)
                    qT = P.tmp('qTt', [128, 8, 128], BF, 1)[0:64, :, :]
                    self.transposes([c['qp'][:, m, :] for m in range(8)], qT)
                    self.transposes([c['kb'][:, h * 64:(h + 1) * 64] for h in range(4)], A_kT64[:, :, tsl])
                    c['qT'] = qT

                def dfC(t, c, mid=None):
                    qT, zs = c['qT'], c['zs']
                    Os = [self.psum_next('O'), self.psum_next('O')]
                    jobs = []
                    for h in range(4):
                        for i in range(2):
                            keys = causal_keys(lambda kt, h=h: A_kT64[:, h, kt * 128:(kt + 1) * 128],
                                               lambda kt, h=h: A_V4[:, kt, h, :], t)
                            jobs.append((qT[:, 2 * h + i, :], keys, 32 ** -0.5, Os[i][:, h * 65:(h + 1) * 65]))
                    self.attention_batch(jobs)
                    O13 = Os[0][:, 0:260].rearrange("p (h c) -> p h c", c=65)
                    O23 = Os[1][:, 0:260].rearrange("p (h c) -> p h c", c=65)
                    r1 = P.tmp('rsl', [128, 4], F32)
                    P.recip(r1, O13[:, :, 64])
                    r2 = P.tmp('rsl2', [128, 4], F32)
                    P.recip(r2, O23[:, :, 64])
                    r2l = P.tmp('fs', [128, 4], F32)
                    P.ts('dve', r2l, r2, neg_lam[:, 0:1], None, ALU.mult)
                    o1 = P.tmp('acc', [128, 256], F32)
                    P.tt('dve', v3(o1, 64), O13[:, :, 0:64], bc_last(r1, 64), ALU.mult)
                    o2 = P.tmp('t2', [128, 256], F32)
                    P.tt('dve', v3(o2, 64), O23[:, :, 0:64], bc_last(r2l, 64), ALU.mult)
                    od = P.tmp('acc', [128, 256], F32)
                    P.tt('pool', od, o1, o2, ALU.add)
                    on = P.tmp('t2', [128, 256], F32)
                    self.rms(od, 4, 64, sublns, on)
                    yb = P.tmp('yb', [128, 256], BF, 2)
                    P.tt('dve', yb, on, zs, ALU.mult)
                    if l == 0:
                        self.dump(f"y1_{t}", yb, BF)
                    c['yb'] = yb

                run_pipeline(dfA, None, dfB, dfC, lambda t, c: out_proj(t, c, lastg == 1), 1)

            if 2 in self.groups:
                set_pools(True)
                load_group_w(1932, 672, 2)
                P.memset('pool', A_V4, 1.0)
                if l == 0:
                    bpads = [P.sb(f"bpad{i}", [128, 512], BF) for i in range(2)]
                    for bp in bpads:
                        P.memset('pool', bp, 0.0)

                def mlA(t, hTt, hook=None):
                    pa = self.psum_next('proj')
                    proj(pa, 416, 0, hTt)
                    sq_a = P.tmp('sq', [128, 384], F32, 1)
                    rs = self.rms_stats([(pa[:, 0:256], sq_a[:, 0:256], 256), (pa[:, 256:384], sq_a[:, 256:384], 128)])
                    cqb = P.tmp('b384', [128, 384], BF, 2)[:, 0:256]
                    self.rms_apply(pa[:, 0:256], rs[:, 0:1], 1, 256, cq_g, cqb)
                    ckvb = P.tmp('ckvb', [128, 128], BF, 1)
                    self.rms_apply(pa[:, 256:384], rs[:, 1:2], 1, 128, ckv_g, ckvb)
                    krr = P.tmp('krr', [128, 32], F32, 1)
                    self.rope(v3(pa[:, 384:416], 32), cos16[:, t, :], sin16[:, t, :], v3(krr, 32))
                    if hook is not None:
                        hook()
                    pb = self.psum_next('proj')
                    proj(pb, 256, 416, hTt)
                    zs = P.tmp('zs', [128, 256], F32, 2)
                    self.silu(pb[:, 0:256], 256, zs)
                    return dict(cqb=cqb, ckvb=ckvb, krr=krr, zs=zs)

                def mlB(t, c):
                    cqb, ckvb, krr = c['cqb'], c['ckvb'], c['krr']
                    cqT = P.tmp('cqT', [128, 2, 128], BF, 1)
                    self.transposes([cqb[:, 0:128], cqb[:, 128:256]], cqT)
                    ckvT = P.tmp('ckvT', [128, 128], BF, 1)
                    self.transposes([ckvb], ckvT)
                    pq = self.psum_next('proj')
                    for cc in range(2):
                        P.mm(pq[:, 0:384], cqT[:, cc, :], Wuq[:, cc, :], start=(cc == 0), stop=(cc == 1))
                    pq3 = pq[:, 0:384].rearrange("p (h c) -> p h c", c=96)
                    qc = P.tmp('f384', [128, 384], F32, 2)
                    qc3 = v3(qc, 96)
                    P.copy('act', qc3[:, :, 0:64], pq3[:, :, 0:64])
                    self.rope(pq3[:, :, 64:96], cos16[:, t, :], sin16[:, t, :], qc3[:, :, 64:96])
                    pkv = self.psum_next('proj')
                    P.mm(pkv, ckvT, Wukv, start=True, stop=True)
                    pkv3 = v3(pkv, 128)
                    kc_ = P.tmp('f384', [128, 384], F32, 2)
                    kc3 = v3(kc_, 96)
                    P.copy('act', kc3[:, :, 0:64], pkv3[:, :, 0:64])
                    P.copy('dve', kc3[:, :, 64:96], bc_mid(krr, 4))
                    P.copy('act', A_V4[:, t, :, 0:64], pkv3[:, :, 64:128])
                    sq_a = P.tmp('sq', [128, 384], F32, 1)
                    sq_b = P.tmp('rmst', [128, 384], F32, 1)
                    rs = self.rms_stats([(qc, sq_a), (kc_, sq_b)], 96)
                    qb = bpads[0]
                    self.rms_apply(qc, rs[:, 0:4], 4, 96, mla_g[:, 0:96], None, out3=v3(qb, 128)[:, :, 0:96], tname='sq')
                    kb = bpads[1]
                    self.rms_apply(kc_, rs[:, 4:8], 4, 96, mla_g[:, 96:192], None, out3=v3(kb, 128)[:, :, 0:96])

                def mlB2(t, c):
                    tsl = slice(t * 128, (t + 1) * 128)
                    qb, kb = bpads[0], bpads[1]
                    qT = P.tmp('qTt', [128, 8, 128], BF, 1)[:, 0:4, :]
                    self.transposes([qb[:, h * 128:(h + 1) * 128] for h in range(4)], qT)
                    self.transposes([kb[:, h * 128:(h + 1) * 128] for h in range(4)], A_kT96[:, :, tsl])
                    c['qT'] = qT

                def mlC(t, c, mid=None):
                    qT, zs = c['qT'], c['zs']
                    OA = self.psum_next('O')
                    jobs = []
                    for h in range(4):
                        keys = causal_keys(lambda kt, h=h: A_kT96[:, h, kt * 128:(kt + 1) * 128],
                                           lambda kt, h=h: A_V4[:, kt, h, :], t)
                        jobs.append((qT[:, h, :], keys, 96 ** -0.5, OA[:, h * 65:(h + 1) * 65]))
                    self.attention_batch(jobs[:2])
                    if mid is not None:
                        mid()
                    self.attention_batch(jobs[2:])
                    OA3 = OA[:, 0:260].rearrange("p (h c) -> p h c", c=65)
                    r1 = P.tmp('rsl', [128, 4], F32)
                    P.recip(r1, OA3[:, :, 64])
                    o1 = P.tmp('acc', [128, 256], F32)
                    P.tt('dve', v3(o1, 64), OA3[:, :, 0:64], bc_last(r1, 64), ALU.mult)
                    yb = P.tmp('yb', [128, 256], BF, 2)
                    P.tt('dve', yb, o1, zs, ALU.mult)
                    if l == 0:
                        self.dump(f"y2_{t}", yb, BF)
                    c['yb'] = yb

                run_pipeline(mlA, mlB, mlB2, mlC, lambda t, c: out_proj(t, c, lastg == 2), 2)

            if 3 in self.groups:
                set_pools(True)
                load_group_w(2604, 512, 3)
                P.memset('pool', A_Vm, 1.0)
                for mt in range(2):
                    mx = P.tmp('big', [128, 1024], F32, 1)
                    P.dma('sp', mx, mem[mt * 128:(mt + 1) * 128, :])
                    junk = P.tmp('qTt', [128, 8, 128], BF, 1).rearrange("p a b -> p (a b)")
                    ss = P.tmp('ss', [128, 8], F32, 3)
                    P.act(junk, mx, AF.Square, accum=ss[:, 0:1])
                    ms = P.tmp('ms', [128, 8], F32, 3)
                    P.ts('dve', ms[:, 0:1], ss[:, 0:1], 1.0 / D, EPS, ALU.mult, ALU.add)
                    rs = P.tmp('rs', [128, 8], F32, 3)
                    P.tt('pool', rs[:, 0:1], ms[:, 0:1], self.neghalf[:, 0:1], ALU.pow)
                    hb = P.tmp('hb', [128, 1024], BF, 1)
                    P.ts('dve', hb, mx, rs[:, 0:1], None, ALU.mult)
                    pt = self.psum_next('tr')
                    for c in range(8):
                        P.tr(pt[:, c * 128:(c + 1) * 128], hb[:, c * 128:(c + 1) * 128], self.ident_bf)
                    P.tt('dve', A_memT[:, :, mt * 128:(mt + 1) * 128], pt.rearrange("p (c t) -> p c t", t=128),
                         bc_last(memgT, 128), ALU.mult)
                for mt in range(2):
                    pk = self.psum_next('proj')
                    for c in range(8):
                        P.mm(pk, A_memT[:, c, mt * 128:(mt + 1) * 128], WIN[:, c, 512:1024], start=(c == 0), stop=(c == 7))
                    kb = P.tmp('b384', [128, 384], BF, 2)[:, 0:256]
                    self.rms(pk[:, 0:256], 4, 64, memqk_g[:, 64:128], kb)
                    self.transposes([kb[:, h * 64:(h + 1) * 64] for h in range(4)], A_kTm[:, :, mt * 128:(mt + 1) * 128])
                    P.copy('act', A_Vm[:, mt, :, 0:64], v3(pk[:, 256:512], 64))

                def mmA(t, hTt, hook=None):
                    pa = self.psum_next('proj')
                    proj(pa, 256, 0, hTt)
                    qb = P.tmp('b384', [128, 384], BF, 2)[:, 0:256]
                    self.rms(pa[:, 0:256], 4, 64, memqk_g[:, 0:64], qb)
                    if hook is not None:
                        hook()
                    pz = self.psum_next('proj')
                    proj(pz, 256, 256, hTt)
                    zs = P.tmp('zs', [128, 256], F32, 2)
                    self.silu(pz[:, 0:256], 256, zs)
                    return dict(qb=qb, zs=zs)

                def mmB(t, c):
                    qT = P.tmp('qTt', [128, 8, 128], BF, 1)[0:64, 0:4, :]
                    self.transposes([c['qb'][:, h * 64:(h + 1) * 64] for h in range(4)], qT)
                    c['qT'] = qT

                def mmC(t, c, mid=None):
                    qT, zs = c['qT'], c['zs']
                    OA = self.psum_next('O')
                    jobs = []
                    for h in range(4):
                        keys = [dict(kT=A_kTm[:, h, kt * 128:(kt + 1) * 128], v=A_Vm[:, kt, h, :], nk=128, biases=[])
                                for kt in range(2)]
                        jobs.append((qT[:, h, :], keys, 0.125, OA[:, h * 65:(h + 1) * 65]))
                    self.attention_batch(jobs)
                    OA3 = OA[:, 0:260].rearrange("p (h c) -> p h c", c=65)
                    r1 = P.tmp('rsl', [128, 4], F32)
                    P.recip(r1, OA3[:, :, 64])
                    o1 = P.tmp('acc', [128, 256], F32)
                    P.tt('dve', v3(o1, 64), OA3[:, :, 0:64], bc_last(r1, 64), ALU.mult)
                    yb = P.tmp('yb', [128, 256], BF, 2)
                    P.tt('dve', yb, o1, zs, ALU.mult)
                    if l == 0:
                        self.dump(f"y3_{t}", yb, BF)
                    c['yb'] = yb

                run_pipeline(mmA, None, mmB, mmC, lambda t, c: out_proj(t, c, lastg == 3), 3)

        P.finalize()
        P.close()
        return nc


_INPUT_NAMES = ["x", "mem", "norm_gain", "w_in", "w_out", "nsa_qk_gain", "nsa_cmp_pe", "nsa_w_cmp",
                "diff_qk_gain", "diff_lambda", "diff_subln_gain", "mla_cq_gain", "mla_ckv_gain", "mla_w_uq",
                "mla_w_ukv", "mla_qk_gain", "mem_norm_gain", "mem_w_kv", "mem_qk_gain"]


def run_model(inputs, debug=None, nlayers=DEPTH, groups=(0, 1, 2, 3), ncores=8):
    nc = bass.Bass("TRN2", target_bir_lowering=False)
    m = Model(nc, debug=debug, nlayers=nlayers, groups=groups)
    m.build()
    arrs = {k: np.ascontiguousarray(np.asarray(inputs[k], dtype=np.float32)) for k in _INPUT_NAMES}
    in_maps = []
    for b in range(ncores):
        d = {k: arrs[k] for k in _INPUT_NAMES if k not in ("x", "mem")}
        d["x"] = np.ascontiguousarray(arrs["x"][b])
        d["mem"] = np.ascontiguousarray(arrs["mem"][b])
        in_maps.append(d)
    res = run_bass_kernel_spmd(nc, in_maps, core_ids=list(range(ncores)))
    return res, m


def kernel(**inputs):
    res, m = run_model(inputs)
    outs = [np.asarray(res.results[b]["out"], dtype=np.float32) for b in range(8)]
    return np.stack(outs, axis=0)
```
